# Optimizing a Trainium2 kernel written in Bass

```python
import math
import jax, jax.numpy as jnp
from jax import lax
import numpy as np

D_MODEL = 2048
BATCH = 4
SEQ = 8192
DEPTH = 1
DEC_BATCH = 16
DEC_SEQ = 16
PAST_LEN = 1024

CHUNK = 64
RET_HEADS = 8
RET_QK_DIM = 256
RET_V_DIM = D_MODEL // RET_HEADS
RET_ROPE_THETA = 10000.0
ATT_HEADS = 32
ATT_KV_HEADS = 4
ATT_HEAD_DIM = D_MODEL // ATT_HEADS
ATT_GROUP = ATT_HEADS // ATT_KV_HEADS
WINDOW = 128
BAND_PREV = WINDOW // CHUNK
ROPE_DIM = ATT_HEAD_DIM // 4
ROPE_THETA = 500000.0
D_FF = ((8 * D_MODEL // 3 + 255) // 256) * 256
ALPHA = (2.0 * DEPTH) ** 0.25
BETA = (8.0 * DEPTH) ** -0.25
LN_EPS = 1e-5
NEG_INF = -1e30
PROJ_WIDTHS = (RET_HEADS * RET_QK_DIM, RET_HEADS * RET_QK_DIM, RET_HEADS * RET_V_DIM, RET_HEADS * RET_V_DIM,
               ATT_HEADS * ATT_HEAD_DIM, ATT_KV_HEADS * ATT_HEAD_DIM, ATT_KV_HEADS * ATT_HEAD_DIM,
               D_MODEL, D_MODEL)
PROJ_WIDTH = sum(PROJ_WIDTHS)

kernel_name = "hybrid_retention_swa_sink_streaming_step"


def layernorm(z, g, b):
    z32 = z.astype(jnp.float32)
    mu = jnp.mean(z32, axis=-1, keepdims=True)
    var = jnp.mean(jnp.square(z32 - mu), axis=-1, keepdims=True)
    return ((z32 - mu) * lax.rsqrt(var + LN_EPS) * g + b).astype(z.dtype)


def rotary(x, pos, rot_dim, theta):
    half = rot_dim // 2
    inv_freq = 1.0 / (theta ** (jnp.arange(half, dtype=jnp.float32) / half))
    ang = pos.astype(jnp.float32)[:, None] * inv_freq[None, :]
    cos = jnp.cos(ang)[None, :, None, :].astype(x.dtype)
    sin = jnp.sin(ang)[None, :, None, :].astype(x.dtype)
    x1, x2, rest = x[..., :half], x[..., half:rot_dim], x[..., rot_dim:]
    return jnp.concatenate([x1 * cos - x2 * sin, x1 * sin + x2 * cos, rest], axis=-1)


def ret_log_gamma():
    return jnp.log1p(-jnp.exp2(-5.0 - jnp.arange(RET_HEADS, dtype=jnp.float32)))


def retention_block(q, k, v, s_prev, log_gamma):
    L = q.shape[2]
    idx = jnp.arange(L, dtype=jnp.float32)
    diff = idx[:, None] - idx[None, :]
    decay = jnp.where(diff >= 0, jnp.exp(log_gamma[:, None, None] * jnp.maximum(diff, 0.0)), 0.0)
    inner = jnp.einsum('bhld,bhmd->bhlm', q, k) * decay
    o = jnp.einsum('bhlm,bhmv->bhlv', inner, v)
    q_dec = jnp.exp(log_gamma[:, None] * (idx + 1.0))
    o = o + jnp.einsum('bhld,bhdv->bhlv', q, s_prev) * q_dec[None, :, :, None]
    k_dec = jnp.exp(log_gamma[:, None] * (L - 1.0 - idx))
    s_new = s_prev * jnp.exp(log_gamma * L)[None, :, None, None] + \
        jnp.einsum('bhld,bhlv->bhdv', k * k_dec[None, :, :, None].astype(k.dtype), v)
    return o, s_new


def retention_prompt(q, k, v):
    B, T, H, dk = q.shape
    dv = v.shape[-1]
    nc = T // CHUNK
    log_gamma = ret_log_gamma()

    def to_chunks(a):
        return a.reshape(B, nc, CHUNK, H, a.shape[-1]).transpose(1, 0, 3, 2, 4)

    def step(s, qkv):
        o, s = retention_block(qkv[0], qkv[1], qkv[2], s, log_gamma)
        return s, o

    s0 = jnp.zeros((B, H, dk, dv), jnp.float32)
    s_fin, o = lax.scan(step, s0, (to_chunks(q), to_chunks(k), to_chunks(v)))
    o = o.transpose(1, 0, 3, 2, 4).reshape(B, T, H, dv)
    return o, s_fin


def retention_sample(q, k, v, state):
    tr = lambda a: a.transpose(0, 2, 1, 3)
    o, s_new = retention_block(tr(q), tr(k), tr(v), state.astype(jnp.float32), ret_log_gamma())
    return tr(o), s_new


def sink_softmax(scores, sink):
    sink = jnp.broadcast_to(sink, scores.shape[:-1] + (1,))
    p = jax.nn.softmax(jnp.concatenate([scores, sink], axis=-1), axis=-1)
    return p[..., :-1]


def swa_prompt(q, k, v, sinks):
    B, T, Hq, hd = q.shape
    nc = T // CHUNK
    nb = BAND_PREV + 1
    qb = q.reshape(B, nc, CHUNK, ATT_KV_HEADS, ATT_GROUP, hd)

    def band(a):
        a = a.reshape(B, nc, CHUNK, ATT_KV_HEADS, hd)
        a = jnp.concatenate([jnp.zeros((B, BAND_PREV, CHUNK, ATT_KV_HEADS, hd), a.dtype), a], axis=1)
        return jnp.concatenate([a[:, j:j + nc] for j in range(nb)], axis=2)

    kb, vb = band(k), band(v)
    s = jnp.einsum('bncxgd,bnkxd->bnxgck', qb, kb).astype(jnp.float32)
    key_chunk = jnp.arange(nc)[:, None] - BAND_PREV + (jnp.arange(nb * CHUNK) // CHUNK)[None, :]
    valid = key_chunk >= 0
    s = jnp.where(valid[None, :, None, None, None, :], s, NEG_INF)
    p = sink_softmax(s, sinks.reshape(1, 1, ATT_KV_HEADS, ATT_GROUP, 1, 1).astype(jnp.float32))
    o = jnp.einsum('bnxgck,bnkxd->bncxgd', p.astype(vb.dtype), vb)
    return o.reshape(B, T, Hq, hd), (k[:, T - WINDOW:], v[:, T - WINDOW:])


def swa_sample(q, k, v, cache_k, cache_v, sinks):
    B, L, Hq, hd = q.shape
    kk = jnp.concatenate([cache_k.astype(k.dtype), k], axis=1)
    vv = jnp.concatenate([cache_v.astype(v.dtype), v], axis=1)
    qg = q.reshape(B, L, ATT_KV_HEADS, ATT_GROUP, hd)
    s = jnp.einsum('blxgd,bkxd->bxglk', qg, kk).astype(jnp.float32)
    p = sink_softmax(s, sinks.reshape(1, ATT_KV_HEADS, ATT_GROUP, 1, 1).astype(jnp.float32))
    o = jnp.einsum('bxglk,bkxd->blxgd', p.astype(vv.dtype), vv)
    return o.reshape(B, L, Hq, hd), (k, v)


def mixer_projections(h, pos, w_in):
    B, T, _ = h.shape
    proj = jnp.einsum('btd,de->bte', h, w_in)
    points, acc = [], 0
    for w in PROJ_WIDTHS[:-1]:
        acc += w
        points.append(acc)
    q_r, k_r, v_r, g_r, q_a, k_a, v_a, gate_r, gate_a = jnp.split(proj, points, axis=-1)
    q_r = rotary(q_r.reshape(B, T, RET_HEADS, RET_QK_DIM), pos, RET_QK_DIM, RET_ROPE_THETA)
    k_r = rotary(k_r.reshape(B, T, RET_HEADS, RET_QK_DIM), pos, RET_QK_DIM, RET_ROPE_THETA) * (RET_QK_DIM ** -0.5)
    v_r = v_r.reshape(B, T, RET_HEADS, RET_V_DIM)
    q_a = rotary(q_a.reshape(B, T, ATT_HEADS, ATT_HEAD_DIM), pos, ROPE_DIM, ROPE_THETA) * (ATT_HEAD_DIM ** -0.5)
    k_a = rotary(k_a.reshape(B, T, ATT_KV_HEADS, ATT_HEAD_DIM), pos, ROPE_DIM, ROPE_THETA)
    v_a = v_a.reshape(B, T, ATT_KV_HEADS, ATT_HEAD_DIM)
    return q_r, k_r, v_r, g_r, q_a, k_a, v_a, gate_r, gate_a


def trunk_layer(x, c, pos, ret_fn, att_fn, w_ada, b_ada, w_in, gn_g, w_o, ln1_g, ln1_b,
                w_ffn_gate, w_ffn_up, w_ffn_down, ln2_g, ln2_b):
    B, T, D = x.shape
    mods = jnp.einsum('bd,de->be', jax.nn.silu(c), w_ada) + b_ada
    sh_a, sc_a, gt_a, sh_f, sc_f, gt_f = jnp.split(mods[:, None, :], 6, axis=-1)
    h = x * (1.0 + sc_a) + sh_a
    q_r, k_r, v_r, g_r, q_a, k_a, v_a, gate_r, gate_a = mixer_projections(h, pos, w_in)
    o_ret, ret_state = ret_fn(q_r, k_r, v_r)
    o32 = o_ret.astype(jnp.float32)
    mu = jnp.mean(o32, axis=-1, keepdims=True)
    var = jnp.mean(jnp.square(o32 - mu), axis=-1, keepdims=True)
    gn = ((o32 - mu) * lax.rsqrt(var + LN_EPS)).reshape(B, T, D) * gn_g
    ret_branch = jax.nn.silu(g_r) * gn.astype(x.dtype)
    o_att, kv_state = att_fn(q_a, k_a, v_a)
    att_branch = o_att.reshape(B, T, D)
    merged = jax.nn.sigmoid(gate_r) * ret_branch + jax.nn.sigmoid(gate_a) * att_branch
    mix = jnp.einsum('btd,de->bte', merged, w_o)
    x1 = layernorm(ALPHA * x + gt_a * mix, ln1_g, ln1_b)
    h2 = x1 * (1.0 + sc_f) + sh_f
    ff = jax.nn.silu(jnp.einsum('btd,df->btf', h2, w_ffn_gate)) * jnp.einsum('btd,df->btf', h2, w_ffn_up)
    ff = jnp.einsum('btf,fd->btd', ff, w_ffn_down)
    x2 = layernorm(ALPHA * x1 + gt_f * ff, ln2_g, ln2_b)
    return x2, ret_state, kv_state


def setup_inputs(seed: int = 0) -> dict:
    key = jax.random.key(seed)
    ks = jax.random.split(key, 24)
    nrm = lambda k, shape, scale: jax.random.normal(k, shape, jnp.float32) * scale
    win_rows = min(WINDOW, PAST_LEN)
    col_scale = jnp.concatenate([jnp.full((w,), BETA if i in (2, 6) else 1.0, jnp.float32)
                                 for i, w in enumerate(PROJ_WIDTHS)]) * (D_MODEL ** -0.5)
    return {
        "x_prompt": nrm(ks[0], (BATCH, SEQ, D_MODEL), 1.0),
        "x_sample": nrm(ks[1], (DEC_BATCH, DEC_SEQ, D_MODEL), 1.0),
        "c_prompt": nrm(ks[2], (BATCH, D_MODEL), 1.0),
        "c_sample": nrm(ks[3], (DEC_BATCH, D_MODEL), 1.0),
        "cache_attn_k": nrm(ks[4], (DEPTH, DEC_BATCH, win_rows, ATT_KV_HEADS, ATT_HEAD_DIM), 1.0),
        "cache_attn_v": nrm(ks[5], (DEPTH, DEC_BATCH, win_rows, ATT_KV_HEADS, ATT_HEAD_DIM), BETA),
        "state_ret": nrm(ks[6], (DEPTH, DEC_BATCH, RET_HEADS, RET_QK_DIM, RET_V_DIM), 0.1),
        "w_ada": nrm(ks[7], (DEPTH, D_MODEL, 6 * D_MODEL), 0.5 * D_MODEL ** -0.5),
        "b_ada": nrm(ks[8], (DEPTH, 6 * D_MODEL), 0.02),
        "w_in": nrm(ks[9], (DEPTH, D_MODEL, PROJ_WIDTH), 1.0) * col_scale,
        "gn_g": 1.0 + nrm(ks[10], (DEPTH, D_MODEL), 0.02),
        "attn_sinks": nrm(ks[11], (DEPTH, ATT_HEADS), 0.5),
        "w_o": nrm(ks[12], (DEPTH, D_MODEL, D_MODEL), BETA * D_MODEL ** -0.5),
        "ln1_g": 1.0 + nrm(ks[13], (DEPTH, D_MODEL), 0.02),
        "ln1_b": nrm(ks[14], (DEPTH, D_MODEL), 0.02),
        "w_ffn_gate": nrm(ks[15], (DEPTH, D_MODEL, D_FF), BETA * D_MODEL ** -0.5),
        "w_ffn_up": nrm(ks[16], (DEPTH, D_MODEL, D_FF), BETA * D_MODEL ** -0.5),
        "w_ffn_down": nrm(ks[17], (DEPTH, D_FF, D_MODEL), BETA * D_FF ** -0.5),
        "ln2_g": 1.0 + nrm(ks[18], (DEPTH, D_MODEL), 0.02),
        "ln2_b": nrm(ks[19], (DEPTH, D_MODEL), 0.02),
    }


def reference(x_prompt, x_sample, c_prompt, c_sample, cache_attn_k, cache_attn_v, state_ret,
              w_ada, b_ada, w_in, gn_g, attn_sinks, w_o, ln1_g, ln1_b,
              w_ffn_gate, w_ffn_up, w_ffn_down, ln2_g, ln2_b):
    pos_p = jnp.arange(x_prompt.shape[1])
    pos_s = PAST_LEN + jnp.arange(x_sample.shape[1])
    y_p, y_s = x_prompt, x_sample
    kp_l, vp_l, sp_l, ks_l, vs_l, ss_l = [], [], [], [], [], []
    for l in range(DEPTH):
        weights = (w_ada[l], b_ada[l], w_in[l], gn_g[l], w_o[l], ln1_g[l], ln1_b[l],
                   w_ffn_gate[l], w_ffn_up[l], w_ffn_down[l], ln2_g[l], ln2_b[l])
        sink_l = attn_sinks[l]
        y_p, s_p, (k_p, v_p) = trunk_layer(
            y_p, c_prompt, pos_p, retention_prompt,
            lambda q, k, v: swa_prompt(q, k, v, sink_l), *weights)
        ck, cv, sr = cache_attn_k[l], cache_attn_v[l], state_ret[l]
        y_s, s_s, (k_s, v_s) = trunk_layer(
            y_s, c_sample, pos_s,
            lambda q, k, v: retention_sample(q, k, v, sr),
            lambda q, k, v: swa_sample(q, k, v, ck, cv, sink_l), *weights)
        kp_l.append(k_p); vp_l.append(v_p); sp_l.append(s_p)
        ks_l.append(k_s); vs_l.append(v_s); ss_l.append(s_s)
    new_attn_k_prompt = jnp.stack(kp_l)
    new_attn_v_prompt = jnp.stack(vp_l)
    new_state_ret_prompt = jnp.stack(sp_l)
    new_attn_k_sample = jnp.stack(ks_l)
    new_attn_v_sample = jnp.stack(vs_l)
    new_state_ret_sample = jnp.stack(ss_l)
    return (y_p, y_s, new_attn_k_prompt, new_attn_v_prompt, new_state_ret_prompt,
            new_attn_k_sample, new_attn_v_sample, new_state_ret_sample)
```

```python
from contextlib import ExitStack
import numpy as np
import concourse.bass as bass
import concourse.mybir as mybir
from concourse.bass_utils import run_bass_kernel_spmd

F32 = mybir.dt.float32
BF16 = mybir.dt.bfloat16
AF = mybir.ActivationFunctionType
ALU = mybir.AluOpType

D = 2048
KC = 16
DFF = 5632
NH = 8
LN_EPS = 1e-5
ALPHA = 2.0 ** 0.25
PAST = 1024
MASKV = -30000.0
ENG_ATTR = {'pe': 'tensor', 'act': 'scalar', 'dve': 'vector', 'pool': 'gpsimd', 'sp': 'sync'}
GAM = [1.0 - 2.0 ** (-5.0 - h) for h in range(NH)]


class Buf:
    def __init__(self, name, t):
        self.name = name
        self.t = t
        self.writers = {}
        self.readers = {}


class Rec:
    def __init__(self, nc):
        self.nc = nc
        self.stack = ExitStack()
        self.streams = {e: [] for e in ENG_ATTR}
        self.count = {e: 0 for e in ENG_ATTR}
        self.seen = {e: {} for e in ENG_ATTR}
        self.awaited = {e: set() for e in ENG_ATTR}
        self.dma_keys = {}

    def sbuf(self, name, shape, dtype):
        return Buf(name, self.stack.enter_context(self.nc.sbuf_tensor(name, shape, dtype)))

    def psum(self, name, shape, dtype):
        return Buf(name, self.stack.enter_context(self.nc.psum_tensor(name, shape, dtype)))

    def _collect(self, eng, reads, writes):
        need = {}
        for b in reads:
            for k, v in b.writers.items():
                if need.get(k, -1) < v:
                    need[k] = v
        for b in writes:
            for d in (b.writers, b.readers):
                for k, v in d.items():
                    if need.get(k, -1) < v:
                        need[k] = v
        waits = []
        seen = self.seen[eng]
        for k, v in need.items():
            if k[0] == 'eng' and k[1] == eng and eng == 'pe':
                continue
            if seen.get(k, -1) >= v:
                continue
            seen[k] = v
            waits.append((k, v))
            if k[0] == 'eng':
                self.awaited[k[1]].add(v)
        return waits

    def op(self, eng, fn, reads=(), writes=()):
        waits = self._collect(eng, reads, writes)
        idx = self.count[eng]
        self.count[eng] += 1
        key = ('eng', eng)
        for b in reads:
            b.readers[key] = idx
        for b in writes:
            b.writers[key] = idx
        self.streams[eng].append(('op', idx, fn, waits))

    def dma(self, q, out, in_, reads=(), writes=()):
        waits = self._collect(q, reads, writes)
        if writes:
            key = ('dma', writes[0].name + '_ld_' + q)
        else:
            key = ('dma', reads[0].name + '_st_' + q)
        val = self.dma_keys.get(key, 0) + 16
        self.dma_keys[key] = val
        for b in reads:
            b.readers[key] = val
        for b in writes:
            b.writers[key] = val
        self.streams[q].append(('dma', key, val, out, in_, waits))

    def handoff(self, olds, news):
        for nb_ in news:
            for ob in olds:
                for d in (ob.writers, ob.readers):
                    for k, v in d.items():
                        if nb_.writers.get(k, -1) < v:
                            nb_.writers[k] = v

    def finish(self):
        nc = self.nc
        final = []
        for k, v in self.dma_keys.items():
            if self.seen['sp'].get(k, -1) < v:
                final.append((k, v))
        for e in ('pe', 'act', 'dve', 'pool'):
            if self.count[e] > 0:
                last = self.count[e] - 1
                self.awaited[e].add(last)
                final.append((('eng', e), last))
        self.streams['sp'].append(('end', final))
        rank = {e: {idx: i + 1 for i, idx in enumerate(sorted(s))} for e, s in self.awaited.items()}
        sems = {}
        for e in ENG_ATTR:
            sems[('eng', e)] = self.stack.enter_context(nc.semaphore('s_' + e))
        for k in self.dma_keys:
            sems[k] = self.stack.enter_context(nc.semaphore('d_' + k[1]))

        def emit_waits(engobj, waits):
            for k, v in waits:
                engobj.wait_ge(sems[k], rank[k[1]][v] if k[0] == 'eng' else v)

        def replay(ename):
            def f(engobj):
                aw = self.awaited[ename]
                mysem = sems[('eng', ename)]
                for ent in self.streams[ename]:
                    if ent[0] == 'op':
                        _, idx, fn, waits = ent
                        emit_waits(engobj, waits)
                        inst = fn(engobj)
                        if idx in aw:
                            inst.then_inc(mysem, 1)
                    elif ent[0] == 'dma':
                        _, key, val, out, in_, waits = ent
                        emit_waits(engobj, waits)
                        engobj.dma_start(out=out, in_=in_).then_inc(sems[key], 16)
                    else:
                        emit_waits(engobj, ent[1])
            return f

        with nc.Block() as block:
            block.sync(replay('sp'))
            block.tensor(replay('pe'))
            block.scalar(replay('act'))
            block.vector(replay('dve'))
            block.gpsimd(replay('pool'))
        self.stack.close()


C_QR, C_KR, C_VR, C_GR, C_QA, C_KA, C_VA, C_GTR, C_GTA, C_KALT = 0, 2048, 4096, 6144, 8192, 10240, 10496, 10752, 12800, 14848
WIN_COLS = 14848 + 256


def build(NP, NSUB):
    TT = NSUB * 128
    TC = NP * TT
    nc = bass.Bass("TRN2", target_bir_lowering=False)

    def din(name, shape):
        return nc.dram_tensor(name, shape, F32, kind="ExternalInput").ap()

    def dout(name, shape):
        return nc.dram_tensor(name, shape, F32, kind="ExternalOutput").ap()

    x_d = din("x", [TC, D]); xp_d = din("xp", [TC, D]); xs_d = din("xs", [32, D])
    flag_d = din("flag", [128, 1]); cT_d = din("cT", [128, KC * 3])
    wada_d = din("w_ada", [D, 6 * D]); badaT_d = din("b_adaT", [128, 96]); bada_d = din("b_ada", [1, 6 * D])
    win_d = din("w_in", [D, WIN_COLS]); wo_d = din("w_o", [D, D])
    wg_d = din("w_g", [D, DFF]); wu_d = din("w_u", [D, DFF]); wd_d = din("w_d", [DFF, D])
    vec_d = din("vecs", [5, D])
    lnT_d = din("lnT", [128, 2 * KC])
    sink_d = din("sinks", [1, 32])
    ckT_d = din("ckT", [128, 2 * 2 * 2 * 128])
    cv_d = din("cv", [128, 2 * 256])
    st_d = din("state", [2, NH, 256, 256])
    rtab_d = din("rtab", [2, 128, TC]); rtabp_d = din("rtabp", [2, 128, TC]); rtabs_d = din("rtabs", [2, 128, 32])
    atab_d = din("atab", [2, 128, TC]); atabp_d = din("atabp", [2, 128, TC]); atabs_d = din("atabs", [2, 128, 32])
    decT_d = din("decT", [128, NH * 128]); qdec_d = din("qdec", [128, NH * 128]); kdec_d = din("kdec", [128, NH])
    decTs_d = din("decTs", [32, NH * 32]); qdecs_d = din("qdecs", [128, NH * 2 * 32]); kdecs_d = din("kdecs", [32, NH * 2])
    mask_d = din("masks", [128, 2 * 128]); masks_d = din("maskss", [128, 3 * 32])
    id_d = din("ident", [128, 128]); psw_d = din("pswap", [128, 128])
    sel_d = din("sel", [3, 128 + 32])

    y_d = dout("y", [TC, D]); ys_d = dout("ys", [32, D])
    ko_d = dout("kout", [128, 256]); vo_d = dout("vout", [128, 256]); so_d = dout("sout", [NH, 256, 256])
    kso_d = dout("ksout", [32, 256]); vso_d = dout("vsout", [32, 256]); sso_d = dout("ssout", [2, NH, 256, 256])

    R = Rec(nc)
    sb = R.sbuf
    NW = 4
    W = [sb(f"W{i}", [128, KC, 512], BF16) for i in range(NW)]
    NPS = 4
    PS = [R.psum(f"ps{i}", [128, 512], F32) for i in range(NPS)]
    PSA = [R.psum(f"psa{i}", [128, 512], F32) for i in range(2)]
    PT = [R.psum(f"pt{i}", [128, 1024], BF16) for i in range(2)]
    st8 = {'w': 0, 'p': 0, 't': 0, 'po': 0}
    PM = {'ce': 'dve', 'q': 'sp'}

    def nb():
        st8['p'] += 1
        return PS[st8['p'] % NPS]

    def nbt():
        st8['t'] += 1
        return PT[st8['t'] % 2]

    xtok = sb("xtok", [128, NSUB, D], F32)
    hT = sb("hT", [128, KC, TT], BF16)
    mT = hT
    h2T = hT
    A1 = R.stack.enter_context(nc.sbuf_tensor("A1", [128, 44 * TT], BF16))
    ffT = Buf("ffT", A1[:, :].rearrange("p (k t) -> p k t", k=44))
    o_ = 0
    merged = Buf("merged", A1[:, o_:o_ + NSUB * D].rearrange("p (s d) -> p s d", s=NSUB)); o_ += NSUB * D
    comb = Buf("comb", A1[:, o_:o_ + NSUB * 1024].bitcast(F32).rearrange("p (s c) -> p s c", s=NSUB)); o_ += NSUB * 1024
    sga = Buf("sga", A1[:, o_:o_ + NSUB * 512].rearrange("p (s c) -> p s c", s=NSUB)); o_ += NSUB * 512
    qT = Buf("qT", A1[:, o_:o_ + 4 * TT].rearrange("p (a b t) -> p a b t", a=2, b=2)); o_ += 4 * TT
    kT = Buf("kT", A1[:, o_:o_ + 4 * TT].rearrange("p (a b t) -> p a b t", a=2, b=2)); o_ += 4 * TT
    qdT = Buf("qdT", A1[:, o_:o_ + 4 * TT].rearrange("p (a b t) -> p a b t", a=2, b=2)); o_ += 4 * TT
    vtok = Buf("vtok", A1[:, o_:o_ + NSUB * 512].rearrange("p (s c) -> p s c", s=NSUB)); o_ += NSUB * 512
    assert o_ == 44 * TT
    mixer_bufs = [merged, comb, sga, qT, kT, qdT, vtok]
    t1 = sb("t1", [128, TT], F32); t2 = sb("t2", [128, TT], F32)
    rtab = sb("rtab_s", [128, 2, TT], F32); atab = sb("atab_s", [128, 2, TT], F32)
    ktok = sb("ktok", [128, 256], BF16); iT = sb("iT", [128, 128], BF16)
    ktokA = sb("ktokA", [128, 2 * NSUB, 256], BF16); iTA = sb("iTA", [128, 2 * NSUB, 128], BF16)
    S32 = sb("S32", [128, NH, 2, 256], F32); Sbf = sb("Sbf", [128, NH, 2, 256], BF16)
    qaTs = [sb(f"qaT{c}", [128, TT], BF16) for c in range(4)]; q32 = sb("q32", [128, TT], F32)
    q32s = [q32, sb("q32B", [128, TT], F32), sb("q32C", [128, TT], F32)]

    kaT = sb("kaT", [128, 2, 2, 128 + TT], BF16)
    ka32 = sb("ka32", [128, 2, TT], F32)
    vext = sb("vext", [128, NSUB + 1, 4, 65], BF16)
    va32 = sb("va32", [128, NSUB, 256], F32)
    pTs = sb("pTs", [128, 512], BF16)
    pTs2 = [pTs, sb("pTsB", [128, 512], BF16)]
    rec = sb("rec", [128, 4], F32); r2 = sb("r2", [128, 256], F32)
    gsil = sb("gsil", [128, 512], F32); gsig = sb("gsig", [128, 512], F32)
    t3 = gsil; t4 = gsig
    r3 = t1; r4 = t2
    stats = sb("stats", [128, 4, 6], F32); statsL = sb("statsL", [128, NSUB, 4, 6], F32); mv = sb("mv", [128, 2], F32); sd = sb("sd", [128, 1], F32); rstd = sb("rstd", [128, 1], F32)
    osb = sb("osb", [128, 256], F32)
    mhalf = sb("mhalf", [128, 1], F32)
    decT = sb("decT_s", [128, NH, 128], F32); qdec = sb("qdec_s", [128, NH, 128], F32); kdec = sb("kdec_s", [128, NH], F32)
    decTs = sb("decTs_s", [32, NH, 32], F32); qdecs = sb("qdecs_s", [128, NH, 2, 32], F32); kdecs = sb("kdecs_s", [32, NH, 2], F32)
    maskb = sb("maskb", [128, 2, 128], BF16); masksb = sb("masksb", [128, 3, 32], BF16)
    idb = sb("idb", [128, 128], BF16); id32 = sb("id32", [128, 128], F32); psw = sb("psw", [128, 128], F32)
    sel = sb("sel_s", [3, 160], F32)
    flag = sb("flag_s", [128, 1], F32)
    esink = sb("esink", [128, 32], F32)
    bc = {"gng": sb("bc_gng", [128, D], BF16)}
    bcb = sb("bcb", [128, D], BF16)
    VEC_ROW = {"l1g": 1, "l1b": 2, "l2g": 3, "l2b": 4}

    vecb = nc.dram_tensor("vecb", [5, D], BF16).ap()
    VB = Buf("vecb", None)

    def bc_load(name):
        i = VEC_ROW[name]
        if PM['q'] == 'sp':
            R.dma('sp', bcb.t[:], vecb[i:i + 1, :].partition_broadcast(128), reads=[VB], writes=[bcb])
        else:
            R.dma('pool', bcb.t[:], vec_d[i:i + 1, :].partition_broadcast(128), writes=[bcb])
        return bcb
    gtAp = sb("gtAp", [128, D], BF16); gtFp = sb("gtFp", [128, D], BF16)
    gtAs = gtAp; gtFs = gtFp
    cT = sb("cT_s", [128, KC, 3], F32); scT = sb("scT", [128, KC, 3], BF16)
    modT = sb("modT", [128, 64, 3], F32)
    badaT = sb("badaT", [128, 96], F32); lnT = sb("lnT_s", [128, 2, KC], F32)
    G2 = sb("G2", [128, KC, 3], F32); B2 = sb("B2", [128, KC, 3], F32)
    gtrow = sb("gtrow", [3, 512], F32); brow = sb("brow", [3, 512], F32)
    ckTb = sb("ckTb", [128, 2, 2, 2, 128], BF16)
    cvext = sb("cvext", [128, 2, 4, 65], BF16)
    ktoks = sb("ktoks", [32, 2, 256], BF16)

    def mm(out, lhsT, rhs, start, stop, reads, writes):
        R.op('pe', lambda e: e.matmul(out, lhsT, rhs, start=start, stop=stop), reads, writes)

    def tp(out, in_, ident, reads, writes):
        R.op('pe', lambda e: e.transpose(out, in_, ident), reads, writes)

    def act(out, in_, func, reads, writes, bias=None, scale=None):
        kw = {}
        if bias is not None:
            kw['bias'] = bias
        if scale is not None:
            kw['scale'] = scale
        R.op('act', lambda e: e.activation(out, in_, func, **kw), reads, writes)

    def tt(out, in0, in1, op, reads, writes, eng='dve'):
        R.op(eng, lambda e: e.tensor_tensor(out, in0, in1, op), reads, writes)

    def ts(out, in0, s1, s2, op0, op1, reads, writes, eng='dve'):
        if op1 is None:
            R.op(eng, lambda e: e.tensor_scalar(out, in0, s1, None, op0), reads, writes)
        else:
            R.op(eng, lambda e: e.tensor_scalar(out, in0, s1, s2, op0, op1), reads, writes)

    def stt(out, in0, scalar, in1, op0, op1, reads, writes):
        R.op('dve', lambda e: e.scalar_tensor_tensor(out, in0, scalar, in1, op0, op1), reads, writes)

    def cp(out, in_, reads, writes, eng='dve'):
        R.op(eng, lambda e: e.tensor_copy(out, in_), reads, writes)

    def ld(out, in_, buf, q='sp'):
        R.dma(q, out, in_, writes=[buf])

    reg = {}
    order = []
    tot = [0]

    def register(key, src, nk, ncols, grp):
        reg[key] = (tot[0], nk, ncols, grp)
        order.append((key, src, nk, ncols, grp))
        tot[0] += nk * ncols

    for hp in range(4):
        register(('in', C_KR + hp * 512), win_d[:, C_KR + hp * 512: C_KR + (hp + 1) * 512], KC, 512, 0)
        register(('in', C_VR + hp * 512), win_d[:, C_VR + hp * 512: C_VR + (hp + 1) * 512], KC, 512, 0)
    register(('in', C_KA), win_d[:, C_KA: C_KA + 512], KC, 512, 0)
    register(('in', C_KALT), win_d[:, C_KALT: C_KALT + 256], KC, 256, 0)
    for x_ in range(4):
        register(('in', C_GTA + x_ * 512), win_d[:, C_GTA + x_ * 512: C_GTA + (x_ + 1) * 512], KC, 512, 1)
        register(('in', C_QA + x_ * 512), win_d[:, C_QA + x_ * 512: C_QA + (x_ + 1) * 512], KC, 512, 1)
    for hp in range(4):
        for c0 in (C_GR, C_GTR, C_QR):
            register(('in', c0 + hp * 512), win_d[:, c0 + hp * 512: c0 + (hp + 1) * 512], KC, 512, 2)
    for nn in range(4):
        register(('o', nn), wo_d[:, nn * 512:(nn + 1) * 512], KC, 512, 3)
    for f in range(11):
        register(('g', f), wg_d[:, f * 512:(f + 1) * 512], KC, 512, 4)
        register(('u', f), wu_d[:, f * 512:(f + 1) * 512], KC, 512, 4)
    for nn in range(4):
        for kg, nk in ((0, 16), (1, 16), (2, 12)):
            register(('d', kg, nn), wd_d[kg * 2048: kg * 2048 + nk * 128, nn * 512:(nn + 1) * 512], nk, 512, 5)
    wsc = nc.dram_tensor("wsc", [128, tot[0]], BF16).ap()
    WG = [Buf(f"wgrp{i}", None) for i in range(6)]

    def convert(groups):
        for (key, src, nk, ncols, grp) in order:
            if grp in groups:
                off = reg[key][0]
                R.dma('pool', wsc[:, off:off + nk * ncols].rearrange("p (k c) -> p k c", k=nk),
                      src.rearrange("(k p) c -> p k c", p=128), writes=[WG[grp]])

    def wload(key, src, nk, ncols):
        st8['w'] += 1
        wb = W[st8['w'] % NW]
        if key is None:
            R.dma('pool', wb.t[:, 0:nk, 0:ncols], src.rearrange("(k p) c -> p k c", p=128), writes=[wb])
        else:
            off, nk2, nc2, grp = reg[key]
            assert nk2 == nk and nc2 == ncols
            R.dma('sp', wb.t[:, 0:nk, 0:ncols], wsc[:, off:off + nk * ncols].rearrange("p (k c) -> p k c", k=nk), reads=[WG[grp]], writes=[wb])
        return wb

    R.dma('pool', vecb, vec_d, writes=[VB])
    convert((0,))
    ld(cT.t[:], cT_d.rearrange("p (k b) -> p k b", b=3), cT)
    ld(badaT.t[:], badaT_d, badaT)
    ld(lnT.t[:], lnT_d.rearrange("p (a k) -> p a k", a=2), lnT)
    ld(flag.t[:], flag_d, flag)
    ld(decT.t[:], decT_d.rearrange("p (h l) -> p h l", h=NH), decT)
    ld(qdec.t[:], qdec_d.rearrange("p (h l) -> p h l", h=NH), qdec)
    ld(kdec.t[:], kdec_d, kdec)
    ld(decTs.t[:], decTs_d.rearrange("p (h l) -> p h l", h=NH), decTs)
    ld(qdecs.t[:], qdecs_d.rearrange("p (h b l) -> p h b l", h=NH, b=2), qdecs)
    ld(kdecs.t[:], kdecs_d.rearrange("p (h b) -> p h b", h=NH), kdecs)
    ld(id32.t[:], id_d, id32)
    ld(psw.t[:], psw_d, psw)
    ld(sel.t[:], sel_d, sel)
    ld(maskb.t[:], mask_d.rearrange("p (a k) -> p a k", a=2), maskb, q='pool')
    ld(masksb.t[:], masks_d.rearrange("p (a k) -> p a k", a=3), masksb, q='pool')
    ld(idb.t[:], id_d, idb, q='pool')
    ld(bc["gng"].t[:], vec_d[0:1, :].partition_broadcast(128), bc["gng"], q='pool')
    ld(esink.t[:], sink_d.partition_broadcast(128), esink)
    act(esink.t[:], esink.t[:], AF.Exp, [esink], [esink])
    act(scT.t[:], cT.t[:], AF.Silu, [cT], [scT])

    gts = nc.dram_tensor("gts", [2, 32, D], BF16).ap()
    GS = Buf("gts", None)

    def build_gates():
        for j, (seg, dst) in enumerate(((2, gtAp), (5, gtFp))):
            for half in range(4):
                wt = wload(None, wada_d[:, seg * D + half * 512: seg * D + (half + 1) * 512], KC, 512)
                ld(brow.t[:], bada_d[0:1, seg * D + half * 512: seg * D + (half + 1) * 512].partition_broadcast(3), brow)
                p = nb()
                for k in range(KC):
                    mm(p.t[0:3, :], scT.t[:, k, :], wt.t[:, k, :], k == 0, k == KC - 1, [scT, wt], [p])
                tt(gtrow.t[:], p.t[0:3, :], brow.t[:], ALU.add, [p, brow], [gtrow])
                p2 = nb()
                mm(p2.t[:, :], sel.t[:, 0:128], gtrow.t[:], True, True, [sel, gtrow], [p2])
                act(dst.t[:, half * 512:(half + 1) * 512], p2.t[:, :], AF.Copy, [p2], [dst])
                p3 = nb()
                mm(p3.t[0:32, :], sel.t[:, 128:160], gtrow.t[:], True, True, [sel, gtrow], [p3])
                act(pTs.t[0:32, :], p3.t[0:32, :], AF.Copy, [p3], [pTs])
                R.dma('sp', gts[j, :, half * 512:(half + 1) * 512], pTs.t[0:32, :], reads=[pTs], writes=[GS])

    for seg in (0, 1, 3, 4):
        for half in range(4):
            wt = wload(None, wada_d[:, seg * D + half * 512: seg * D + (half + 1) * 512], KC, 512)
            base = {0: 0, 1: 16, 3: 32, 4: 48}[seg]
            p = nb()
            for c in range(4):
                for k in range(KC):
                    mm(p.t[:, c * 4:c * 4 + 3], wt.t[:, k, c * 128:(c + 1) * 128], scT.t[:, k, :], k == 0, k == KC - 1, [scT, wt], [p])
            for c in range(4):
                jj = seg * 16 + half * 4 + c
                ts(modT.t[:, base + half * 4 + c, :], p.t[:, c * 4:c * 4 + 3], badaT.t[:, jj:jj + 1], None, ALU.add, None, [p, badaT], [modT])
    ts(modT.t[:, 16:32, :], modT.t[:, 16:32, :], 1.0, None, ALU.add, None, [modT], [modT])
    ts(modT.t[:, 48:64, :], modT.t[:, 48:64, :], 1.0, None, ALU.add, None, [modT], [modT])
    for b in range(3):
        tt(G2.t[:, :, b], modT.t[:, 48:64, b], lnT.t[:, 0, :], ALU.mult, [modT, lnT], [G2])
        tt(B2.t[:, :, b], modT.t[:, 48:64, b], lnT.t[:, 1, :], ALU.mult, [modT, lnT], [B2])
        tt(B2.t[:, :, b], B2.t[:, :, b], modT.t[:, 32:48, b], ALU.add, [modT, B2], [B2])
    build_gates()
    convert((1, 2, 3, 4, 5))
    R.op('dve', lambda e: e.memset(vext.t[:], 1.0), [], [vext])
    R.op('dve', lambda e: e.memset(mhalf.t[:], -0.5), [], [mhalf])
    ts(bc["gng"].t[:], bc["gng"].t[:], 0.25, None, ALU.mult, None, [bc["gng"]], [bc["gng"]])
    R.op('dve', lambda e: e.memset(cvext.t[:], 1.0), [], [cvext])
    R.op('dve', lambda e: e.memset(S32.t[:], 0.0), [], [S32])
    R.op('dve', lambda e: e.memset(Sbf.t[:], 0.0), [], [Sbf])

    def load_x_make_hT(xsrc, ntok, bcols, xb=None, hdst=None):
        xb = xb or xtok
        hdst = hdst or hT
        nsub = (ntok + 127) // 128
        for s in range(nsub):
            n = min(128, ntok - s * 128)
            ld(xb.t[0:n, s, :], xsrc[s * 128:s * 128 + n, :], xb)
        for k in range(KC):
            p = nb()
            for s in range(nsub):
                n = min(128, ntok - s * 128)
                tp(p.t[:, s * 128:s * 128 + n], xb.t[0:n, s, k * 128:(k + 1) * 128], id32.t[0:n, 0:n], [xb, id32], [p])
            for (c0, ncol, b) in bcols:
                act(hdst.t[:, k, c0:c0 + ncol], p.t[:, c0:c0 + ncol], AF.Identity, [p, modT], [hdst],
                    bias=modT.t[:, k, b:b + 1], scale=modT.t[:, 16 + k, b:b + 1])

    def wslot_as_x():
        st8['w'] += 1
        wb = W[st8['w'] % NW]
        v = Buf(wb.name, wb.t[:].rearrange("p k c -> p (k c)").bitcast(F32)[:, 0:NSUB * D].rearrange("p (s d) -> p s d", s=NSUB))
        v.writers = wb.writers
        v.readers = wb.readers
        return v

    def proj_fm(wt, c, src, ntok):
        p = nb()
        for k in range(KC):
            mm(p.t[:, 0:ntok], wt.t[:, k, c * 128:(c + 1) * 128], src.t[:, k, 0:ntok], k == 0, k == KC - 1, [wt, src], [p])
        return p

    def proj_tm(wt, s, n, src, c0, ncols):
        p = nb()
        for k in range(KC):
            mm(p.t[0:n, 0:ncols], src.t[:, k, s * 128:s * 128 + n], wt.t[:, k, c0:c0 + ncols], k == 0, k == KC - 1, [wt, src], [p])
        return p

    def rot_ret(pa, pb, dst, hh, ntok, ceng=None):
        ceng = ceng or PM['ce']
        cs, sn = rtab.t[:, 0, 0:ntok], rtab.t[:, 1, 0:ntok]
        tt(t1.t[:, 0:ntok], pa.t[:, 0:ntok], cs, ALU.mult, [pa, rtab], [t1])
        tt(t2.t[:, 0:ntok], pb.t[:, 0:ntok], sn, ALU.mult, [pb, rtab], [t2])
        tt(t3.t[:, 0:ntok], pa.t[:, 0:ntok], sn, ALU.mult, [pa, rtab], [t3])
        tt(t4.t[:, 0:ntok], pb.t[:, 0:ntok], cs, ALU.mult, [pb, rtab], [t4])
        tt(dst.t[:, hh, 0, 0:ntok], t1.t[:, 0:ntok], t2.t[:, 0:ntok], ALU.subtract, [t1, t2], [dst], eng=ceng)
        tt(dst.t[:, hh, 1, 0:ntok], t3.t[:, 0:ntok], t4.t[:, 0:ntok], ALU.add, [t3, t4], [dst], eng=ceng)

    def rot_att(p, ntok, out32=None, outb=None):
        act(q32.t[:, 0:ntok], p.t[:, 0:ntok], AF.Copy, [p], [q32])
        p2 = nb()
        mm(p2.t[:, 0:ntok], psw.t[:], q32.t[:, 0:ntok], True, True, [psw, q32], [p2])
        tt(t1.t[:, 0:ntok], q32.t[:, 0:ntok], atab.t[:, 0, 0:ntok], ALU.mult, [q32, atab], [t1])
        tt(t2.t[:, 0:ntok], p2.t[:, 0:ntok], atab.t[:, 1, 0:ntok], ALU.mult, [p2, atab], [t2])
        if out32 is not None:
            tt(out32, t1.t[:, 0:ntok], t2.t[:, 0:ntok], ALU.add, [t1, t2], [ka32])
            cp(outb, out32, [ka32], [kaT])
        else:
            tt(outb[0], t1.t[:, 0:ntok], t2.t[:, 0:ntok], ALU.add, [t1, t2], [outb[1]])

    def ret_kv_proj(hp, src, ntok, nsub_list, ceng='dve'):
        wt = wload(('in', C_KR + hp * 512), win_d[:, C_KR + hp * 512: C_KR + (hp + 1) * 512], KC, 512)
        for hh in range(2):
            pa = proj_fm(wt, 2 * hh, src, ntok)
            pb = proj_fm(wt, 2 * hh + 1, src, ntok)
            rot_ret(pa, pb, kT, hh, ntok, ceng)
        wt = wload(('in', C_VR + hp * 512), win_d[:, C_VR + hp * 512: C_VR + (hp + 1) * 512], KC, 512)
        for (s, n) in nsub_list:
            p = proj_tm(wt, s, n, src, 0, 512)
            act(vtok.t[0:n, s, :], p.t[0:n, :], AF.Copy, [p], [vtok])

    def state_update(h, hh, s):
        pt = nbt()
        for half in range(2):
            tp(pt.t[:, half * 128:(half + 1) * 128], kT.t[:, hh, half, s * 128:(s + 1) * 128], idb.t[:], [kT, idb], [pt])
        ts(ktok.t[:], pt.t[:, 0:256], kdec.t[:, h:h + 1], None, ALU.mult, None, [pt, kdec], [ktok])
        p = nb()
        for half in range(2):
            mm(p.t[:, half * 256:(half + 1) * 256], ktok.t[:, half * 128:(half + 1) * 128], vtok.t[:, s, hh * 256:(hh + 1) * 256], True, True, [ktok, vtok], [p])
        stt(S32.t[:, h, :, :], S32.t[:, h, :, :], GAM[h] ** 128, p.t[:].rearrange("p (a v) -> p a v", a=2), ALU.mult, ALU.add, [S32, p], [S32])
        act(Sbf.t[:, h, :, :], S32.t[:, h, :, :], AF.Copy, [S32], [Sbf])

    def attn_kv_proj(src, ntok, nsub_list, tabsrc, want_out):
        wt = wload(('in', C_KA), win_d[:, C_KA: C_KA + 512], KC, 512)
        wt2 = wload(('in', C_KALT), win_d[:, C_KALT: C_KALT + 256], KC, 256)
        for j in range(2):
            p = proj_fm(wt, j, src, ntok)
            rot_att(p, ntok, out32=ka32.t[:, j, 0:ntok], outb=kaT.t[:, 0, j, 128:128 + ntok])
        for j in range(2):
            p = proj_fm(wt2, j, src, ntok)
            rot_att(p, ntok, outb=(kaT.t[:, 1, j, 128:128 + ntok], kaT))
        for (s, n) in nsub_list:
            p = proj_tm(wt, s, n, src, 256, 256)
            act(va32.t[0:n, s, :], p.t[0:n, 0:256], AF.Copy, [p], [va32])
            cp(vext.t[0:n, 1 + s, :, 0:64], va32.t[0:n, s, :].rearrange("p (x d) -> p x d", x=4), [va32], [vext])

    def shift_prev(ntok):
        cp(kaT.t[:, :, :, 0:128], kaT.t[:, :, :, ntok:ntok + 128], [kaT], [kaT])
        cp(vext.t[:, 0, :, :], vext.t[:, ntok // 128, :, :], [vext], [vext])

    def ln_block_stats(s, n, q):
        R.op('dve', lambda e: e.bn_stats(statsL.t[0:n, s, q, :], xtok.t[0:n, s, q * 512:(q + 1) * 512]), [xtok], [statsL])

    def layer_norm_stats(s, n):
        R.op('dve', lambda e: e.bn_aggr(mv.t[0:n, :], statsL.t[0:n, s, :, :].rearrange("p a b -> p (a b)")), [statsL], [mv])
        rstd_from_mv(n)

    def rstd_from_mv(n):
        ts(sd.t[0:n, :], mv.t[0:n, 1:2], LN_EPS, None, ALU.add, None, [mv], [sd])
        if PM['ce'] == 'pool':
            tt(rstd.t[0:n, :], sd.t[0:n, :], mhalf.t[0:n, :], ALU.pow, [sd, mhalf], [rstd], eng='pool')
        else:
            act(sd.t[0:n, :], sd.t[0:n, :], AF.Sqrt, [sd], [sd])
            R.op('dve', lambda e: e.reciprocal(rstd.t[0:n, :], sd.t[0:n, :]), [sd], [rstd])

    def tail_phases(ntok, nsub_list, bcols, gtA, gtF, xsrc, ydst, mid_hook=None):
        for k in range(KC):
            pt = nbt()
            for (s, n) in nsub_list:
                tp(pt.t[:, s * 128:s * 128 + n], merged.t[0:n, s, k * 128:(k + 1) * 128], idb.t[0:n, 0:n], [merged, idb], [pt])
            act(mT.t[:, k, 0:ntok], pt.t[:, 0:ntok], AF.Copy, [pt], [mT])
        for (s, n) in nsub_list:
            ld(xtok.t[0:n, s, :], xsrc[s * 128:s * 128 + n, :], xtok)
        for nn in range(4):
            wt = wload(('o', nn), wo_d[:, nn * 512:(nn + 1) * 512], KC, 512)
            for (s, n) in nsub_list:
                p = proj_tm(wt, s, n, mT, 0, 512)
                tt(gsil.t[0:n, :], p.t[0:n, :], gtA.t[0:n, nn * 512:(nn + 1) * 512], ALU.mult, [p, gtA], [gsil])
                stt(xtok.t[0:n, s, nn * 512:(nn + 1) * 512], xtok.t[0:n, s, nn * 512:(nn + 1) * 512], ALPHA, gsil.t[0:n, :], ALU.mult, ALU.add, [xtok, gsil], [xtok])
                ln_block_stats(s, n, nn)
        for (s, n) in nsub_list:
            layer_norm_stats(s, n)
            ts(xtok.t[0:n, s, :], xtok.t[0:n, s, :], mv.t[0:n, 0:1], rstd.t[0:n, 0:1], ALU.subtract, ALU.mult, [xtok, mv, rstd], [xtok])
        for (s, n) in nsub_list:
            for k4 in range(4):
                p = nb()
                for kk in range(4):
                    k = k4 * 4 + kk
                    tp(p.t[:, kk * 128:kk * 128 + n], xtok.t[0:n, s, k * 128:(k + 1) * 128], id32.t[0:n, 0:n], [xtok, id32], [p])
                for kk in range(4):
                    k = k4 * 4 + kk
                    for (c0, ncol, b) in bcols:
                        lo, hi = max(c0, s * 128), min(c0 + ncol, s * 128 + n)
                        if hi > lo:
                            act(h2T.t[:, k, lo:hi], p.t[:, kk * 128 + lo - s * 128: kk * 128 + hi - s * 128], AF.Identity, [p, G2, B2], [h2T],
                                bias=B2.t[:, k, b:b + 1], scale=G2.t[:, k, b:b + 1])
        for nm, op_ in (("l1g", ALU.mult), ("l1b", ALU.add)):
            bb = bc_load(nm)
            for (s, n) in nsub_list:
                tt(xtok.t[0:n, s, :], xtok.t[0:n, s, :], bb.t[0:n, :], op_, [xtok, bb], [xtok], eng=PM['ce'])
        R.handoff(mixer_bufs, [ffT])
        for f in range(11):
            wg = wload(('g', f), wg_d[:, f * 512:(f + 1) * 512], KC, 512)
            wu = wload(('u', f), wu_d[:, f * 512:(f + 1) * 512], KC, 512)
            for c in range(4):
                pg = proj_fm(wg, c, h2T, ntok)
                pu = proj_fm(wu, c, h2T, ntok)
                act(t1.t[:, 0:ntok], pg.t[:, 0:ntok], AF.Tanh, [pg], [t1], scale=0.5)
                stt(t1.t[:, 0:ntok], t1.t[:, 0:ntok], 1.0, pg.t[:, 0:ntok], ALU.add, ALU.mult, [t1, pg], [t1])
                stt(ffT.t[:, f * 4 + c, 0:ntok], t1.t[:, 0:ntok], 0.5, pu.t[:, 0:ntok], ALU.mult, ALU.mult, [t1, pu], [ffT])
        if mid_hook is not None:
            mid_hook()
        for nn in range(4):
            accs = [nb() for _ in nsub_list]
            for kg, nk in ((0, 16), (1, 16), (2, 12)):
                wt = wload(('d', kg, nn), wd_d[kg * 2048: kg * 2048 + nk * 128, nn * 512:(nn + 1) * 512], nk, 512)
                for i, (s, n) in enumerate(nsub_list):
                    for k in range(nk):
                        kk = kg * 16 + k
                        mm(accs[i].t[0:n, :], ffT.t[:, kk, s * 128:s * 128 + n], wt.t[:, k, :], kk == 0, kk == 43, [ffT, wt], [accs[i]])
            for i, (s, n) in enumerate(nsub_list):
                tt(gsil.t[0:n, :], accs[i].t[0:n, :], gtF.t[0:n, nn * 512:(nn + 1) * 512], ALU.mult, [accs[i], gtF], [gsil])
                stt(xtok.t[0:n, s, nn * 512:(nn + 1) * 512], xtok.t[0:n, s, nn * 512:(nn + 1) * 512], ALPHA, gsil.t[0:n, :], ALU.mult, ALU.add, [xtok, gsil], [xtok])
                ln_block_stats(s, n, nn)
        R.handoff([ffT], mixer_bufs)
        for (s, n) in nsub_list:
            layer_norm_stats(s, n)
            ts(xtok.t[0:n, s, :], xtok.t[0:n, s, :], mv.t[0:n, 0:1], rstd.t[0:n, 0:1], ALU.subtract, ALU.mult, [xtok, mv, rstd], [xtok])
        for nm, op_ in (("l2g", ALU.mult), ("l2b", ALU.add)):
            bb = bc_load(nm)
            for (s, n) in nsub_list:
                tt(xtok.t[0:n, s, :], xtok.t[0:n, s, :], bb.t[0:n, :], op_, [xtok, bb], [xtok], eng=PM['ce'])
        for (s, n) in nsub_list:
            R.dma(PM['q'], ydst[s * 128:s * 128 + n, :], xtok.t[0:n, s, :], reads=[xtok])

    full_subs = [(s, 128) for s in range(NSUB)]

    RSTEPS = [(hh, s) for s in range(NSUB) for hh in range(2)]

    def ret_stageA(hp, with_q):
        for i, (hh, s) in enumerate(RSTEPS):
            h = 2 * hp + hh
            blk = slice(s * 128, (s + 1) * 128)
            pt = nbt()
            for half in range(2):
                tp(pt.t[:, half * 128:(half + 1) * 128], kT.t[:, hh, half, blk], idb.t[:], [kT, idb], [pt])
            act(ktokA.t[:, i, :], pt.t[:, 0:256], AF.Copy, [pt, kdec], [ktokA], scale=kdec.t[:, h:h + 1])
            if with_q:
                for half in range(2):
                    tt(qdT.t[:, hh, half, blk], qT.t[:, hh, half, blk], qdec.t[:, h, :], ALU.mult, [qT, qdec], [qdT], eng=PM['ce'])
                pi = nb()
                for half in range(2):
                    mm(pi.t[:, 0:128], kT.t[:, hh, half, blk], qT.t[:, hh, half, blk], half == 0, half == 1, [kT, qT], [pi])
                tt(iTA.t[:, i, :], pi.t[:, 0:128], decT.t[:, h, :], ALU.mult, [pi, decT], [iTA])

    def ret_stepB(hp, i, with_o):
        hh, s = RSTEPS[i]
        h = 2 * hp + hh
        blk = slice(s * 128, (s + 1) * 128)
        if with_o:
            po = nb()
            mm(po.t[:, 0:256], iTA.t[:, i, :], vtok.t[:, s, hh * 256:(hh + 1) * 256], True, False, [iTA, vtok], [po])
            for half in range(2):
                mm(po.t[:, 0:256], qdT.t[:, hh, half, blk], Sbf.t[:, h, half, :], False, half == 1, [qdT, Sbf], [po])
        p = nb()
        for half in range(2):
            mm(p.t[:, half * 256:(half + 1) * 256], ktokA.t[:, i, half * 128:(half + 1) * 128], vtok.t[:, s, hh * 256:(hh + 1) * 256], True, True, [ktokA, vtok], [p])
        stt(S32.t[:, h, :, :], S32.t[:, h, :, :], GAM[h] ** 128, p.t[:].rearrange("p (a v) -> p a v", a=2), ALU.mult, ALU.add, [S32, p], [S32])
        if with_o:
            act(Sbf.t[:, h, :, :], S32.t[:, h, :, :], AF.Copy, [S32], [Sbf])
            R.op('dve', lambda e, po=po: e.bn_stats(stats.t[:, 0, :], po.t[:, 0:256]), [po], [stats])
            R.op('dve', lambda e: e.bn_aggr(mv.t[:], stats.t[:, 0, :]), [stats], [mv])
            rstd_from_mv(128)
            c0 = h * 256
            stt(osb.t[:], po.t[:, 0:256], mv.t[:, 0:1], comb.t[:, s, hh * 256:hh * 256 + 256], ALU.subtract, ALU.mult, [po, mv, comb], [osb])
            stt(merged.t[:, s, c0:c0 + 256], osb.t[:], rstd.t[:, 0:1], merged.t[:, s, c0:c0 + 256], ALU.mult, ALU.add, [osb, rstd, merged], [merged])

    def retB_gen(hp):
        for i in range(len(RSTEPS)):
            ret_stepB(hp, i, True)
            yield

    def ret_proj(hp):
        wt = wload(('in', C_QR + hp * 512), win_d[:, C_QR + hp * 512: C_QR + (hp + 1) * 512], KC, 512)
        for hh in range(2):
            pa = proj_fm(wt, 2 * hh, hT, TT)
            pb = proj_fm(wt, 2 * hh + 1, hT, TT)
            rot_ret(pa, pb, qT, hh, TT)
        wt = wload(('in', C_KR + hp * 512), win_d[:, C_KR + hp * 512: C_KR + (hp + 1) * 512], KC, 512)
        for hh in range(2):
            pa = proj_fm(wt, 2 * hh, hT, TT)
            pb = proj_fm(wt, 2 * hh + 1, hT, TT)
            rot_ret(pa, pb, kT, hh, TT)
        wt = wload(('in', C_VR + hp * 512), win_d[:, C_VR + hp * 512: C_VR + (hp + 1) * 512], KC, 512)
        for (s, n) in full_subs:
            p = proj_tm(wt, s, n, hT, 0, 512)
            act(vtok.t[0:n, s, :], p.t[0:n, :], AF.Copy, [p], [vtok])
        ret_stageA(hp, True)
        wt = wload(('in', C_GR + hp * 512), win_d[:, C_GR + hp * 512: C_GR + (hp + 1) * 512], KC, 512)
        wt2 = wload(('in', C_GTR + hp * 512), win_d[:, C_GTR + hp * 512: C_GTR + (hp + 1) * 512], KC, 512)
        for (s, n) in full_subs:
            p = proj_tm(wt, s, n, hT, 0, 512)
            act(gsil.t[:], p.t[:], AF.Tanh, [p], [gsil], scale=0.5)
            stt(gsil.t[:], gsil.t[:], 1.0, p.t[:], ALU.add, ALU.mult, [gsil, p], [gsil])
            p2 = proj_tm(wt2, s, n, hT, 0, 512)
            act(gsig.t[:], p2.t[:], AF.Tanh, [p2], [gsig], scale=0.5)
            stt(gsil.t[:], gsig.t[:], 1.0, gsil.t[:], ALU.add, ALU.mult, [gsil, gsig], [gsil])
            tt(comb.t[:, s, :], gsil.t[:], bc["gng"].t[:, hp * 512:(hp + 1) * 512], ALU.mult, [gsil, bc["gng"]], [comb], eng=PM['ce'])

    def att_group(x_):
        wt = wload(('in', C_GTA + x_ * 512), win_d[:, C_GTA + x_ * 512: C_GTA + (x_ + 1) * 512], KC, 512)
        for (s, n) in full_subs:
            p = proj_tm(wt, s, n, hT, 0, 512)
            act(sga.t[:, s, :], p.t[:, :], AF.Tanh, [p], [sga], scale=0.5)
            yield
        wt = wload(('in', C_QA + x_ * 512), win_d[:, C_QA + x_ * 512: C_QA + (x_ + 1) * 512], KC, 512)

        def rot_fin(c):
            qb = q32s[c % 3]
            p2 = nb()
            mm(p2.t[:, 0:TT], psw.t[:], qb.t[:, 0:TT], True, True, [psw, qb], [p2])
            tt(r3.t[:, 0:TT], qb.t[:, 0:TT], atab.t[:, 0, 0:TT], ALU.mult, [qb, atab], [r3], eng=PM['ce'])
            tt(r4.t[:, 0:TT], p2.t[:, 0:TT], atab.t[:, 1, 0:TT], ALU.mult, [p2, atab], [r4])
            tt(qaTs[c].t[:, 0:TT], r3.t[:, 0:TT], r4.t[:, 0:TT], ALU.add, [r3, r4], [qaTs[c]], eng=PM['ce'])

        for c in range(4):
            p = proj_fm(wt, c, hT, TT)
            act(q32s[c % 3].t[:, 0:TT], p.t[:, 0:TT], AF.Copy, [p], [q32s[c % 3]])
            if c > 1:
                rot_fin(c - 2)
            yield
        rot_fin(2)
        yield
        rot_fin(3)
        yield
        units = [(s, half, gp) for s in range(NSUB) for half in range(2) for gp in range(2)]
        pos = {}

        def scores(i):
            s, half, gp = units[i]
            psc = nb()
            for g2 in range(2):
                g = half * 4 + gp * 2 + g2
                off = (g % 2) * 64
                ver = 0 if (x_ % 2) == (g % 2) else 1
                j = x_ // 2
                cb = g2 * 256
                for kb in range(2):
                    kc0 = s * 128 + kb * 128
                    mm(psc.t[:, cb + kb * 128: cb + (kb + 1) * 128], kaT.t[off:off + 64, ver, j, kc0:kc0 + 128],
                       qaTs[g // 2].t[off:off + 64, s * 128:(s + 1) * 128], True, False, [kaT, qaTs[g // 2]], [psc])
                    mm(psc.t[:, cb + kb * 128: cb + (kb + 1) * 128], idb.t[:], maskb.t[:, kb, :], False, True, [idb, maskb], [psc])
            act(pTs2[i % 2].t[:], psc.t[:], AF.Exp, [psc], [pTs2[i % 2]], scale=0.125)

        def pv(i):
            s, half, gp = units[i]
            if gp == 0:
                st8['po'] += 1
                pos[(s, half)] = PSA[st8['po'] % 2]
            po = pos[(s, half)]
            pt_ = pTs2[i % 2]
            for g2 in range(2):
                gg = gp * 2 + g2
                for kb in range(2):
                    mm(po.t[:, gg * 65:(gg + 1) * 65], pt_.t[:, g2 * 256 + kb * 128: g2 * 256 + (kb + 1) * 128],
                       vext.t[:, s + kb, x_, :], kb == 0, kb == 1, [pt_, vext], [po])
            if gp == 1:
                h0 = x_ * 8 + half * 4
                pov = po.t[:, 0:260].rearrange("p (g d) -> p g d", g=4)
                tt(rec.t[:], pov[:, :, 64], esink.t[:, h0:h0 + 4], ALU.add, [po, esink], [rec])
                R.op('dve', lambda e: e.reciprocal(rec.t[:], rec.t[:]), [rec], [rec])
                ts(rec.t[:], rec.t[:], 0.5, None, ALU.mult, None, [rec], [rec])
                c0 = h0 * 64
                stt(r2.t[:].rearrange("p (g d) -> p g d", g=4), sga.t[:, s, half * 256:half * 256 + 256].rearrange("p (g d) -> p g d", g=4), 1.0,
                    rec.t[:].unsqueeze(2).broadcast_to([128, 4, 64]), ALU.add, ALU.mult, [sga, rec], [r2])
                tt(merged.t[:, s, c0:c0 + 256].rearrange("p (g d) -> p g d", g=4), pov[:, :, 0:64],
                   r2.t[:].rearrange("p (g d) -> p g d", g=4), ALU.mult, [po, r2], [merged])

        scores(0)
        for i in range(len(units)):
            if i + 1 < len(units):
                scores(i + 1)
            pv(i)
            yield

    def interleave(g1, g2):
        gens = [g for g in (g1, g2) if g is not None]
        while gens:
            for g in list(gens):
                try:
                    next(g)
                except StopIteration:
                    gens.remove(g)

    xpre = Buf("xpre", A1[:, 0:NSUB * 2 * D].bitcast(F32).rearrange("p (s d) -> p s d", s=NSUB))
    hT2 = Buf("hT2", xtok.t[:].rearrange("p s d -> p (s d)").bitcast(BF16)[:, 0:KC * TT].rearrange("p (k t) -> p k t", k=KC))
    hbufs = [hT, hT2]
    load_x_make_hT(xp_d[0:TT, :], TT, [(0, TT, 0)], xb=xpre, hdst=hbufs[0])
    for p_ in range(NP):
        if p_ + 1 < NP:
            load_x_make_hT(xp_d[(p_ + 1) * TT:(p_ + 2) * TT, :], TT, [(0, TT, 0)], xb=xpre, hdst=hbufs[(p_ + 1) % 2])
        hsrc = hbufs[p_ % 2]
        ld(rtab.t[:], rtabp_d[:, :, p_ * TT:(p_ + 1) * TT].rearrange("a p t -> p a t"), rtab)
        for hp in range(4):
            ret_kv_proj(hp, hsrc, TT, full_subs)
            ret_stageA(hp, False)
            for i in range(len(RSTEPS)):
                ret_stepB(hp, i, False)
        if p_ == NP - 1:
            ld(atab.t[:], atabp_d[:, :, p_ * TT:(p_ + 1) * TT].rearrange("a p t -> p a t"), atab)
            attn_kv_proj(hsrc, TT, full_subs, None, False)
            shift_prev(TT)
    R.handoff([xpre], [merged, comb, sga, qT])
    R.handoff([hT2], [xtok])
    ts(S32.t[:].rearrange("p h a v -> p (h a v)"), S32.t[:].rearrange("p h a v -> p (h a v)"), flag.t[:, 0:1], None, ALU.mult, None, [S32, flag], [S32])
    act(Sbf.t[:].rearrange("p h a v -> p (h a v)"), S32.t[:].rearrange("p h a v -> p (h a v)"), AF.Copy, [S32], [Sbf])
    ts(vext.t[:, 0, :, :], vext.t[:, 0, :, :], flag.t[:, 0:1], None, ALU.mult, None, [vext, flag], [vext])

    def make_main_hT(p_):
        load_x_make_hT(x_d[p_ * TT:(p_ + 1) * TT, :], TT, [(0, TT, 0)], xb=wslot_as_x(), hdst=hT)

    make_main_hT(0)
    for p_ in range(NP):
        PM['ce'], PM['q'] = ('dve', 'sp') if p_ == 0 else ('pool', 'pool')
        xsrc = x_d[p_ * TT:(p_ + 1) * TT, :]
        ld(rtab.t[:], rtab_d[:, :, p_ * TT:(p_ + 1) * TT].rearrange("a p t -> p a t"), rtab)
        ld(atab.t[:], atab_d[:, :, p_ * TT:(p_ + 1) * TT].rearrange("a p t -> p a t"), atab)
        attn_kv_proj(hT, TT, full_subs, None, p_ == NP - 1)
        if p_ == NP - 1:
            p = nb()
            for j in range(2):
                tp(p.t[:, j * 128:(j + 1) * 128], ka32.t[:, j, TT - 128:TT], id32.t[:], [ka32, id32], [p])
            act(osb.t[:], p.t[:, 0:256], AF.Copy, [p], [osb])
            R.dma('sp', ko_d, osb.t[:], reads=[osb])
            R.dma('sp', vo_d, va32.t[:, NSUB - 1, :], reads=[va32])
        interleave(att_group(0), None)
        for hp in range(4):
            ret_proj(hp)
            interleave(retB_gen(hp), att_group(hp + 1) if hp < 3 else None)
        shift_prev(TT)
        tail_phases(TT, full_subs, [(0, TT, 0)], gtAp, gtFp, xsrc, y_d[p_ * TT:(p_ + 1) * TT, :],
                    mid_hook=(lambda p_=p_: make_main_hT(p_ + 1)) if p_ + 1 < NP else None)
    R.dma('sp', so_d.rearrange("h (a p) v -> p h a v", p=128), S32.t[:], reads=[S32])

    NS = 32
    ssub = [(0, NS)]
    sb_cols = [(0, 16, 1), (16, 16, 2)]
    ld(ckTb.t[:], ckT_d.rearrange("p (b v j k) -> p b v j k", b=2, v=2, j=2), ckTb, q='pool')
    ld(cvext.t[:, :, :, 0:64], cv_d.rearrange("p (b x d) -> p b x d", b=2, x=4), cvext, q='pool')
    for j_, dst_ in ((0, gtAp), (1, gtFp)):
        R.dma('sp', dst_.t[0:32, :], gts[j_], reads=[GS], writes=[dst_])
    load_x_make_hT(xs_d, NS, sb_cols)
    ld(rtab.t[:, :, 0:NS], rtabs_d.rearrange("a p t -> p a t"), rtab)
    ld(atab.t[:, :, 0:NS], atabs_d.rearrange("a p t -> p a t"), atab)
    wt = wload(('in', C_KA), win_d[:, C_KA: C_KA + 512], KC, 512)
    wt2 = wload(('in', C_KALT), win_d[:, C_KALT: C_KALT + 256], KC, 256)
    for j in range(2):
        p = proj_fm(wt, j, hT, NS)
        rot_att(p, NS, out32=ka32.t[:, j, 0:NS], outb=kaT.t[:, 0, j, 128:128 + NS])
    for j in range(2):
        p = proj_fm(wt2, j, hT, NS)
        rot_att(p, NS, outb=(kaT.t[:, 1, j, 128:128 + NS], kaT))
    p = proj_tm(wt, 0, NS, hT, 256, 256)
    act(va32.t[0:NS, 0, :], p.t[0:NS, 0:256], AF.Copy, [p], [va32])
    cp(vext.t[0:NS, 1, :, 0:64], va32.t[0:NS, 0, :].rearrange("p (x d) -> p x d", x=4), [va32], [vext])
    p = nb()
    for j in range(2):
        tp(p.t[0:NS, j * 128:(j + 1) * 128], ka32.t[:, j, 0:NS], id32.t[:], [ka32, id32], [p])
    act(osb.t[0:NS, :], p.t[0:NS, 0:256], AF.Copy, [p], [osb])
    R.dma('sp', kso_d, osb.t[0:NS, :], reads=[osb])
    R.dma('sp', vso_d, va32.t[0:NS, 0, :], reads=[va32])
    for x_ in range(4):
        wt = wload(('in', C_GTA + x_ * 512), win_d[:, C_GTA + x_ * 512: C_GTA + (x_ + 1) * 512], KC, 512)
        p = proj_tm(wt, 0, NS, hT, 0, 512)
        act(sga.t[0:NS, 0, :], p.t[0:NS, :], AF.Sigmoid, [p], [sga])
        wt = wload(('in', C_QA + x_ * 512), win_d[:, C_QA + x_ * 512: C_QA + (x_ + 1) * 512], KC, 512)
        for c in range(4):
            p = proj_fm(wt, c, hT, NS)
            rot_att(p, NS, outb=(qaTs[c].t[:, 0:NS], qaTs[c]))
        for half in range(2):
            po = PSA[half]
            for g4 in range(4):
                g = half * 4 + g4
                off = (g % 2) * 64
                ver = 0 if (x_ % 2) == (g % 2) else 1
                j = x_ // 2
                psc = nb()
                for kb in range(2):
                    mm(psc.t[:, kb * 32:(kb + 1) * 32], ckTb.t[off:off + 64, kb, ver, j, :], qaTs[g // 2].t[off:off + 64, 0:NS], True, False, [ckTb, qaTs[g // 2]], [psc])
                    mm(psc.t[:, kb * 32:(kb + 1) * 32], idb.t[:], masksb.t[:, kb, :], False, True, [idb, masksb], [psc])
                mm(psc.t[0:NS, 64:96], kaT.t[off:off + 64, ver, j, 128:128 + NS], qaTs[g // 2].t[off:off + 64, 0:NS], True, False, [kaT, qaTs[g // 2]], [psc])
                mm(psc.t[0:NS, 64:96], idb.t[0:NS, 0:NS], masksb.t[0:NS, 2, :], False, True, [idb, masksb], [psc])
                act(pTs.t[:, 0:64], psc.t[:, 0:64], AF.Exp, [psc], [pTs], scale=0.125)
                act(pTs.t[0:NS, 64:96], psc.t[0:NS, 64:96], AF.Exp, [psc], [pTs], scale=0.125)
                for kb in range(2):
                    mm(po.t[0:NS, g4 * 65:(g4 + 1) * 65], pTs.t[:, kb * 32:(kb + 1) * 32], cvext.t[:, kb, x_, :], kb == 0, False, [pTs, cvext], [po])
                mm(po.t[0:NS, g4 * 65:(g4 + 1) * 65], pTs.t[0:NS, 64:96], vext.t[0:NS, 1, x_, :], False, True, [pTs, vext], [po])
            h0 = x_ * 8 + half * 4
            pov = po.t[0:NS, 0:260].rearrange("p (g d) -> p g d", g=4)
            tt(rec.t[0:NS, :], pov[:, :, 64], esink.t[0:NS, h0:h0 + 4], ALU.add, [po, esink], [rec])
            R.op('dve', lambda e: e.reciprocal(rec.t[0:NS, :], rec.t[0:NS, :]), [rec], [rec])
            c0 = h0 * 64
            tt(r2.t[0:NS, :].rearrange("p (g d) -> p g d", g=4), sga.t[0:NS, 0, half * 256:half * 256 + 256].rearrange("p (g d) -> p g d", g=4),
               rec.t[0:NS, :].unsqueeze(2).broadcast_to([NS, 4, 64]), ALU.mult, [sga, rec], [r2])
            tt(merged.t[0:NS, 0, c0:c0 + 256].rearrange("p (g d) -> p g d", g=4), pov[:, :, 0:64],
               r2.t[0:NS, :].rearrange("p (g d) -> p g d", g=4), ALU.mult, [po, r2], [merged])
    for hp in range(4):
        wt = wload(('in', C_GR + hp * 512), win_d[:, C_GR + hp * 512: C_GR + (hp + 1) * 512], KC, 512)
        wt2 = wload(('in', C_GTR + hp * 512), win_d[:, C_GTR + hp * 512: C_GTR + (hp + 1) * 512], KC, 512)
        p = proj_tm(wt, 0, NS, hT, 0, 512)
        act(gsil.t[0:NS, :], p.t[0:NS, :], AF.Silu, [p], [gsil])
        p2 = proj_tm(wt2, 0, NS, hT, 0, 512)
        act(gsig.t[0:NS, :], p2.t[0:NS, :], AF.Sigmoid, [p2], [gsig])
        tt(gsil.t[0:NS, :], gsil.t[0:NS, :], gsig.t[0:NS, :], ALU.mult, [gsil, gsig], [gsil])
        stt(comb.t[0:NS, 0, :], gsil.t[0:NS, :], 4.0, bc["gng"].t[0:NS, hp * 512:(hp + 1) * 512], ALU.mult, ALU.mult, [gsil, bc["gng"]], [comb])
        wt = wload(('in', C_QR + hp * 512), win_d[:, C_QR + hp * 512: C_QR + (hp + 1) * 512], KC, 512)
        for hh in range(2):
            pa = proj_fm(wt, 2 * hh, hT, NS)
            pb = proj_fm(wt, 2 * hh + 1, hT, NS)
            rot_ret(pa, pb, qT, hh, NS)
        ret_kv_proj(hp, hT, NS, ssub)
        for bi in range(2):
            ld(S32.t[:, bi * 2:bi * 2 + 2, :, :], st_d[bi, 2 * hp:2 * hp + 2].rearrange("h (a p) v -> p h a v", p=128), S32)
        act(Sbf.t[:, 0:4, :, :], S32.t[:, 0:4, :, :], AF.Copy, [S32], [Sbf])
        for hh in range(2):
            h = 2 * hp + hh
            pi = nb()
            for half in range(2):
                mm(pi.t[0:NS, 0:NS], kT.t[:, hh, half, 0:NS], qT.t[:, hh, half, 0:NS], half == 0, half == 1, [kT, qT], [pi])
            tt(iT.t[0:NS, 0:NS], pi.t[0:NS, 0:NS], decTs.t[:, h, :], ALU.mult, [pi, decTs], [iT])
            po = nb()
            mm(po.t[0:NS, 0:256], iT.t[0:NS, 0:NS], vtok.t[0:NS, 0, hh * 256:(hh + 1) * 256], True, False, [iT, vtok], [po])
            for bi in range(2):
                for half in range(2):
                    tt(qdT.t[:, hh, half, 0:NS], qT.t[:, hh, half, 0:NS], qdecs.t[:, h, bi, :], ALU.mult, [qT, qdecs], [qdT])
                    mm(po.t[0:NS, 0:256], qdT.t[:, hh, half, 0:NS], Sbf.t[:, bi * 2 + hh, half, :], False, (bi == 1 and half == 1), [qdT, Sbf], [po])
            R.op('dve', lambda e, po=po: e.bn_stats(stats.t[0:NS, 0, :], po.t[0:NS, 0:256]), [po], [stats])
            R.op('dve', lambda e: e.bn_aggr(mv.t[0:NS, :], stats.t[0:NS, 0, :]), [stats], [mv])
            rstd_from_mv(NS)
            c0 = h * 256
            stt(osb.t[0:NS, :], po.t[0:NS, 0:256], mv.t[0:NS, 0:1], comb.t[0:NS, 0, hh * 256:hh * 256 + 256], ALU.subtract, ALU.mult, [po, mv, comb], [osb])
            stt(merged.t[0:NS, 0, c0:c0 + 256], osb.t[0:NS, :], rstd.t[0:NS, 0:1], merged.t[0:NS, 0, c0:c0 + 256], ALU.mult, ALU.add, [osb, rstd, merged], [merged])
            pt = nbt()
            for half in range(2):
                tp(pt.t[0:NS, half * 128:(half + 1) * 128], kT.t[:, hh, half, 0:NS], idb.t[:], [kT, idb], [pt])
            for bi in range(2):
                ts(ktoks.t[:, bi, :], pt.t[0:NS, 0:256], kdecs.t[:, h, bi:bi + 1], None, ALU.mult, None, [pt, kdecs], [ktoks])
                p = nb()
                for half in range(2):
                    mm(p.t[:, half * 256:(half + 1) * 256], ktoks.t[:, bi, half * 128:(half + 1) * 128], vtok.t[0:NS, 0, hh * 256:(hh + 1) * 256], True, True, [ktoks, vtok], [p])
                stt(S32.t[:, bi * 2 + hh, :, :], S32.t[:, bi * 2 + hh, :, :], GAM[h] ** 16, p.t[:].rearrange("p (a v) -> p a v", a=2), ALU.mult, ALU.add, [S32, p], [S32])
        for bi in range(2):
            R.dma('sp', sso_d[bi, 2 * hp:2 * hp + 2].rearrange("h (a p) v -> p h a v", p=128), S32.t[:, bi * 2:bi * 2 + 2, :, :], reads=[S32])
    tail_phases(NS, ssub, sb_cols, gtAs, gtFs, xs_d, ys_d)
    R.finish()
    return nc


def _rot_tables(pos):
    pos = pos.astype(np.float32)
    inv_r = (1.0 / (np.float32(10000.0) ** (np.arange(128, dtype=np.float32) / np.float32(128)))).astype(np.float32)
    ang = (pos[None, :] * inv_r[:, None]).astype(np.float32)
    rt = np.stack([np.cos(ang), np.sin(ang)]).astype(np.float32)
    inv_a = (1.0 / (np.float32(500000.0) ** (np.arange(8, dtype=np.float32) / np.float32(8)))).astype(np.float32)
    anga = (pos[None, :] * inv_a[:, None]).astype(np.float32)
    ca, sa = np.cos(anga).astype(np.float32), np.sin(anga).astype(np.float32)
    C = np.ones((64, pos.shape[0]), np.float32); S = np.zeros((64, pos.shape[0]), np.float32)
    C[0:8] = ca; C[8:16] = ca; S[0:8] = -sa; S[8:16] = sa
    at = np.stack([np.concatenate([C, C]), np.concatenate([S, S])]).astype(np.float32)
    return rt, at


def _const_tables():
    g = np.array(GAM, np.float64)
    l = np.arange(128)
    diff = l[None, :] - l[:, None]
    decT = np.where(diff[:, None, :] >= 0, g[None, :, None] ** np.maximum(diff, 0)[:, None, :], 0.0) / 16.0
    qdec = np.broadcast_to((g[:, None] ** (l[None, :] + 1.0))[None], (128, NH, 128))
    kdec = (g[None, :] ** (127.0 - l[:, None])) / 16.0
    t = np.arange(32)
    same = (t[:, None] // 16) == (t[None, :] // 16)
    d2 = (t[None, :] % 16) - (t[:, None] % 16)
    decTs = np.where((same & (d2 >= 0))[:, None, :], g[None, :, None] ** np.maximum(d2, 0)[:, None, :], 0.0) / 16.0
    qdecs = np.zeros((128, NH, 2, 32)); kdecs = np.zeros((32, NH, 2))
    for bi in range(2):
        m = (t // 16) == bi
        qdecs[:, :, bi, :] = (g[:, None] ** (t[None, :] % 16 + 1.0)) * m[None, :]
        kdecs[:, :, bi] = (g[None, :] ** (15.0 - (t[:, None] % 16))) / 16.0 * m[:, None]
    k = np.arange(128)
    maskP = np.where((k[:, None] < 64) & (k[None, :] >= 64), MASKV, 0.0)
    maskO = np.where((k[:, None] >= 64) & (k[None, :] < 64), MASKV, 0.0)
    masks = np.concatenate([maskP, maskO], axis=1)
    ms = np.zeros((128, 3, 32))
    for bi in range(2):
        ms[:, bi, :] = np.where((t[None, :] // 16) == bi, 0.0, MASKV)
    ms[0:32, 2, :] = np.where(same, 0.0, MASKV)
    ident = np.eye(128)
    psw = np.zeros((128, 128))
    for base in (0, 64):
        for i in range(8):
            psw[base + i + 8, base + i] = 1.0
            psw[base + i, base + i + 8] = 1.0
    sel = np.zeros((3, 160)); sel[0, 0:128] = 1.0; sel[1, 128:144] = 1.0; sel[2, 144:160] = 1.0
    f = lambda a: np.ascontiguousarray(a, dtype=np.float32)
    return dict(decT=f(decT.reshape(128, -1)), qdec=f(qdec.reshape(128, -1)), kdec=f(kdec), decTs=f(decTs.reshape(32, -1)),
                qdecs=f(qdecs.reshape(128, -1)), kdecs=f(kdecs.reshape(32, -1)), masks=f(masks), maskss=f(ms.reshape(128, -1)),
                ident=f(ident), pswap=f(psw), sel=f(sel))


def run(inputs, NP, NSUB, n_cores, trace=False):
    f = lambda a: np.ascontiguousarray(np.asarray(a), dtype=np.float32)
    xpr = f(inputs["x_prompt"]); xsm = f(inputs["x_sample"])
    TC = NP * NSUB * 128
    w_in = f(inputs["w_in"])[0]
    ka = w_in[:, C_KA:C_KA + 256].reshape(D, 4, 64)[:, [1, 0, 3, 2], :].reshape(D, 256)
    shared = dict(
        w_ada=f(inputs["w_ada"])[0], b_adaT=f(f(inputs["b_ada"])[0].reshape(96, 128).T), b_ada=f(inputs["b_ada"]),
        w_in=f(np.concatenate([w_in, ka], axis=1)), w_o=f(inputs["w_o"])[0], w_g=f(inputs["w_ffn_gate"])[0],
        w_u=f(inputs["w_ffn_up"])[0], w_d=f(inputs["w_ffn_down"])[0],
        vecs=f(np.stack([f(inputs[n])[0] for n in ("gn_g", "ln1_g", "ln1_b", "ln2_g", "ln2_b")])),
        lnT=f(np.concatenate([f(inputs["ln1_g"])[0].reshape(KC, 128).T, f(inputs["ln1_b"])[0].reshape(KC, 128).T], axis=1)),
        sinks=f(inputs["attn_sinks"]),
    )
    shared.update(_const_tables())
    rts, ats = _rot_tables(PAST + (np.arange(32) % 16))
    shared["rtabs"] = rts; shared["atabs"] = ats
    ck = f(inputs["cache_attn_k"])[0]; cvv = f(inputs["cache_attn_v"])[0]; stt_ = f(inputs["state_ret"])[0]
    cpr = f(inputs["c_prompt"]); csm = f(inputs["c_sample"])
    in_maps = []
    for c in range(n_cores):
        b, hf = c // 2, c % 2
        m = dict(shared)
        m["x"] = f(xpr[b, hf * TC:(hf + 1) * TC])
        m["xp"] = f(xpr[b, 0:TC]) if hf == 1 else np.zeros((TC, D), np.float32)
        m["xs"] = f(xsm[2 * c:2 * c + 2].reshape(32, D))
        m["flag"] = np.full((128, 1), float(hf), np.float32)
        cc = np.stack([cpr[b], csm[2 * c], csm[2 * c + 1]])
        m["cT"] = f(cc.reshape(3, KC, 128).transpose(2, 1, 0).reshape(128, KC * 3))
        rt, at = _rot_tables(hf * TC + np.arange(TC)); m["rtab"] = rt; m["atab"] = at
        rt, at = _rot_tables(np.arange(TC)); m["rtabp"] = rt; m["atabp"] = at
        ckc = np.zeros((128, 2, 2, 2, 128), np.float32)
        for bi in range(2):
            kk = ck[2 * c + bi]
            for ver in range(2):
                order = [0, 1, 2, 3] if ver == 0 else [1, 0, 3, 2]
                kt = kk[:, order, :].reshape(128, 2, 128)
                ckc[:, bi, ver, :, :] = kt.transpose(2, 1, 0)
        m["ckT"] = f(ckc.reshape(128, -1))
        m["cv"] = f(np.stack([cvv[2 * c + bi].reshape(128, 256) for bi in range(2)], axis=1).reshape(128, 512))
        m["state"] = f(stt_[2 * c:2 * c + 2])
        in_maps.append(m)
    nc = build(NP, NSUB)
    res = run_bass_kernel_spmd(nc, in_maps, core_ids=list(range(n_cores)), trace=trace)
    return res


def assemble(res, n_cores, TC):
    r = res.results
    nb_ = n_cores // 2
    y_p = np.zeros((nb_, 2 * TC, D), np.float32)
    y_s = np.zeros((2 * n_cores, 16, D), np.float32)
    kp = np.zeros((1, nb_, 128, 4, 64), np.float32); vp = np.zeros((1, nb_, 128, 4, 64), np.float32)
    sp = np.zeros((1, nb_, NH, 256, 256), np.float32)
    ks = np.zeros((1, 2 * n_cores, 16, 4, 64), np.float32); vs = np.zeros((1, 2 * n_cores, 16, 4, 64), np.float32)
    ss = np.zeros((1, 2 * n_cores, NH, 256, 256), np.float32)
    for c in range(n_cores):
        b, hf = c // 2, c % 2
        y_p[b, hf * TC:(hf + 1) * TC] = r[c]["y"]
        y_s[2 * c:2 * c + 2] = r[c]["ys"].reshape(2, 16, D)
        ks[0, 2 * c:2 * c + 2] = r[c]["ksout"].reshape(2, 16, 4, 64)
        vs[0, 2 * c:2 * c + 2] = r[c]["vsout"].reshape(2, 16, 4, 64)
        ss[0, 2 * c:2 * c + 2] = r[c]["ssout"]
        if hf == 1:
            kp[0, b] = r[c]["kout"].reshape(128, 4, 64)
            vp[0, b] = r[c]["vout"].reshape(128, 4, 64)
            sp[0, b] = r[c]["sout"]
    return (y_p, y_s, kp, vp, sp, ks, vs, ss)


def kernel(**inputs):
    NSUB = 2
    NP = 4096 // (NSUB * 128)
    res = run(inputs, NP, NSUB, 8)
    return assemble(res, 8, 4096)
```

```python
from contextlib import ExitStack
import numpy as np
import concourse.bass as bass
import concourse.mybir as mybir
from concourse.bass_utils import run_bass_kernel_spmd

F32 = mybir.dt.float32
BF16 = mybir.dt.bfloat16
AF = mybir.ActivationFunctionType
ALU = mybir.AluOpType

D = 2048
KC = 16
DFF = 5632
NH = 8
LN_EPS = 1e-5
ALPHA = 2.0 ** 0.25
PAST = 1024
MASKV = -30000.0
ENG_ATTR = {'pe': 'tensor', 'act': 'scalar', 'dve': 'vector', 'pool': 'gpsimd', 'sp': 'sync'}
GAM = [1.0 - 2.0 ** (-5.0 - h) for h in range(NH)]


class Buf:
    def __init__(self, name, t):
        self.name = name
        self.t = t
        self.writers = {}
        self.readers = {}


class Rec:
    def __init__(self, nc):
        self.nc = nc
        self.stack = ExitStack()
        self.streams = {e: [] for e in ENG_ATTR}
        self.count = {e: 0 for e in ENG_ATTR}
        self.seen = {e: {} for e in ENG_ATTR}
        self.awaited = {e: set() for e in ENG_ATTR}
        self.dma_keys = {}

    def sbuf(self, name, shape, dtype):
        return Buf(name, self.stack.enter_context(self.nc.sbuf_tensor(name, shape, dtype)))

    def psum(self, name, shape, dtype):
        return Buf(name, self.stack.enter_context(self.nc.psum_tensor(name, shape, dtype)))

    def _collect(self, eng, reads, writes):
        need = {}
        for b in reads:
            for k, v in b.writers.items():
                if need.get(k, -1) < v:
                    need[k] = v
        for b in writes:
            for d in (b.writers, b.readers):
                for k, v in d.items():
                    if need.get(k, -1) < v:
                        need[k] = v
        waits = []
        seen = self.seen[eng]
        for k, v in need.items():
            if k[0] == 'eng' and k[1] == eng and eng == 'pe':
                continue
            if seen.get(k, -1) >= v:
                continue
            seen[k] = v
            waits.append((k, v))
            if k[0] == 'eng':
                self.awaited[k[1]].add(v)
        return waits

    def op(self, eng, fn, reads=(), writes=()):
        waits = self._collect(eng, reads, writes)
        idx = self.count[eng]
        self.count[eng] += 1
        key = ('eng', eng)
        for b in reads:
            b.readers[key] = idx
        for b in writes:
            b.writers[key] = idx
        self.streams[eng].append(('op', idx, fn, waits))

    def dma(self, q, out, in_, reads=(), writes=()):
        waits = self._collect(q, reads, writes)
        if writes:
            key = ('dma', writes[0].name + '_ld_' + q)
        else:
            key = ('dma', reads[0].name + '_st_' + q)
        val = self.dma_keys.get(key, 0) + 16
        self.dma_keys[key] = val
        for b in reads:
            b.readers[key] = val
        for b in writes:
            b.writers[key] = val
        self.streams[q].append(('dma', key, val, out, in_, waits))

    def handoff(self, olds, news):
        for nb_ in news:
            for ob in olds:
                for d in (ob.writers, ob.readers):
                    for k, v in d.items():
                        if nb_.writers.get(k, -1) < v:
                            nb_.writers[k] = v

    def finish(self):
        nc = self.nc
        final = []
        for k, v in self.dma_keys.items():
            if self.seen['sp'].get(k, -1) < v:
                final.append((k, v))
        for e in ('pe', 'act', 'dve', 'pool'):
            if self.count[e] > 0:
                last = self.count[e] - 1
                self.awaited[e].add(last)
                final.append((('eng', e), last))
        self.streams['sp'].append(('end', final))
        rank = {e: {idx: i + 1 for i, idx in enumerate(sorted(s))} for e, s in self.awaited.items()}
        sems = {}
        for e in ENG_ATTR:
            sems[('eng', e)] = self.stack.enter_context(nc.semaphore('s_' + e))
        for k in self.dma_keys:
            sems[k] = self.stack.enter_context(nc.semaphore('d_' + k[1]))

        def emit_waits(engobj, waits):
            for k, v in waits:
                engobj.wait_ge(sems[k], rank[k[1]][v] if k[0] == 'eng' else v)

        def replay(ename):
            def f(engobj):
                aw = self.awaited[ename]
                mysem = sems[('eng', ename)]
                for ent in self.streams[ename]:
                    if ent[0] == 'op':
                        _, idx, fn, waits = ent
                        emit_waits(engobj, waits)
                        inst = fn(engobj)
                        if idx in aw:
                            inst.then_inc(mysem, 1)
                    elif ent[0] == 'dma':
                        _, key, val, out, in_, waits = ent
                        emit_waits(engobj, waits)
                        engobj.dma_start(out=out, in_=in_).then_inc(sems[key], 16)
                    else:
                        emit_waits(engobj, ent[1])
            return f

        with nc.Block() as block:
            block.sync(replay('sp'))
            block.tensor(replay('pe'))
            block.scalar(replay('act'))
            block.vector(replay('dve'))
            block.gpsimd(replay('pool'))
        self.stack.close()


C_QR, C_KR, C_VR, C_GR, C_QA, C_KA, C_VA, C_GTR, C_GTA, C_KALT = 0, 2048, 4096, 6144, 8192, 10240, 10496, 10752, 12800, 14848
WIN_COLS = 14848 + 256


def build(NP, NSUB):
    TT = NSUB * 128
    TC = NP * TT
    nc = bass.Bass("TRN2", target_bir_lowering=False)

    def din(name, shape):
        return nc.dram_tensor(name, shape, F32, kind="ExternalInput").ap()

    def dout(name, shape):
        return nc.dram_tensor(name, shape, F32, kind="ExternalOutput").ap()

    x_d = din("x", [TC, D]); xp_d = din("xp", [TC, D]); xs_d = din("xs", [32, D])
    flag_d = din("flag", [128, 1]); cT_d = din("cT", [128, KC * 3])
    wada_d = din("w_ada", [D, 6 * D]); badaT_d = din("b_adaT", [128, 96]); bada_d = din("b_ada", [1, 6 * D])
    win_d = din("w_in", [D, WIN_COLS]); wo_d = din("w_o", [D, D])
    wg_d = din("w_g", [D, DFF]); wu_d = din("w_u", [D, DFF]); wd_d = din("w_d", [DFF, D])
    vec_d = din("vecs", [5, D])
    lnT_d = din("lnT", [128, 2 * KC])
    sink_d = din("sinks", [1, 32])
    ckT_d = din("ckT", [128, 2 * 2 * 2 * 128])
    cv_d = din("cv", [128, 2 * 256])
    st_d = din("state", [2, NH, 256, 256])
    rtab_d = din("rtab", [2, 128, TC]); rtabp_d = din("rtabp", [2, 128, TC]); rtabs_d = din("rtabs", [2, 128, 32])
    atab_d = din("atab", [2, 128, TC]); atabp_d = din("atabp", [2, 128, TC]); atabs_d = din("atabs", [2, 128, 32])
    decT_d = din("decT", [128, NH * 128]); qdec_d = din("qdec", [128, NH * 128]); kdec_d = din("kdec", [128, NH])
    decTs_d = din("decTs", [32, NH * 32]); qdecs_d = din("qdecs", [128, NH * 2 * 32]); kdecs_d = din("kdecs", [32, NH * 2])
    mask_d = din("masks", [128, 2 * 128]); masks_d = din("maskss", [128, 3 * 32])
    id_d = din("ident", [128, 128]); psw_d = din("pswap", [128, 128])
    sel_d = din("sel", [3, 128 + 32])

    y_d = dout("y", [TC, D]); ys_d = dout("ys", [32, D])
    ko_d = dout("kout", [128, 256]); vo_d = dout("vout", [128, 256]); so_d = dout("sout", [NH, 256, 256])
    kso_d = dout("ksout", [32, 256]); vso_d = dout("vsout", [32, 256]); sso_d = dout("ssout", [2, NH, 256, 256])

    R = Rec(nc)
    sb = R.sbuf
    NW = 4
    W = [sb(f"W{i}", [128, KC, 512], BF16) for i in range(NW)]
    NPS = 4
    PS = [R.psum(f"ps{i}", [128, 512], F32) for i in range(NPS)]
    PSA = [R.psum(f"psa{i}", [128, 512], F32) for i in range(2)]
    PT = [R.psum(f"pt{i}", [128, 1024], BF16) for i in range(2)]
    st8 = {'w': 0, 'p': 0, 't': 0, 'po': 0}
    PM = {'ce': 'dve', 'q': 'sp'}

    def nb():
        st8['p'] += 1
        return PS[st8['p'] % NPS]

    def nbt():
        st8['t'] += 1
        return PT[st8['t'] % 2]

    xtok = sb("xtok", [128, NSUB, D], F32)
    hT = sb("hT", [128, KC, TT], BF16)
    mT = hT
    h2T = hT
    A1 = R.stack.enter_context(nc.sbuf_tensor("A1", [128, 44 * TT], BF16))
    ffT = Buf("ffT", A1[:, :].rearrange("p (k t) -> p k t", k=44))
    o_ = 0
    merged = Buf("merged", A1[:, o_:o_ + NSUB * D].rearrange("p (s d) -> p s d", s=NSUB)); o_ += NSUB * D
    comb = Buf("comb", A1[:, o_:o_ + NSUB * 1024].bitcast(F32).rearrange("p (s c) -> p s c", s=NSUB)); o_ += NSUB * 1024
    sga = Buf("sga", A1[:, o_:o_ + NSUB * 512].rearrange("p (s c) -> p s c", s=NSUB)); o_ += NSUB * 512
    qT = Buf("qT", A1[:, o_:o_ + 4 * TT].rearrange("p (a b t) -> p a b t", a=2, b=2)); o_ += 4 * TT
    kT = Buf("kT", A1[:, o_:o_ + 4 * TT].rearrange("p (a b t) -> p a b t", a=2, b=2)); o_ += 4 * TT
    qdT = Buf("qdT", A1[:, o_:o_ + 4 * TT].rearrange("p (a b t) -> p a b t", a=2, b=2)); o_ += 4 * TT
    vtok = Buf("vtok", A1[:, o_:o_ + NSUB * 512].rearrange("p (s c) -> p s c", s=NSUB)); o_ += NSUB * 512
    assert o_ == 44 * TT
    mixer_bufs = [merged, comb, sga, qT, kT, qdT, vtok]
    t1 = sb("t1", [128, TT], F32); t2 = sb("t2", [128, TT], F32)
    rtab = sb("rtab_s", [128, 2, TT], F32); atab = sb("atab_s", [128, 2, TT], F32)
    ktok = sb("ktok", [128, 256], BF16); iT = sb("iT", [128, 128], BF16)
    ktokA = sb("ktokA", [128, 2 * NSUB, 256], BF16); iTA = sb("iTA", [128, 2 * NSUB, 128], BF16)
    S32 = sb("S32", [128, NH, 2, 256], F32); Sbf = sb("Sbf", [128, NH, 2, 256], BF16)
    qaTs = [sb(f"qaT{c}", [128, TT], BF16) for c in range(4)]; q32 = sb("q32", [128, TT], F32)
    q32s = [q32, sb("q32B", [128, TT], F32), sb("q32C", [128, TT], F32)]

    kaT = sb("kaT", [128, 2, 2, 128 + TT], BF16)
    ka32 = sb("ka32", [128, 2, TT], F32)
    vext = sb("vext", [128, NSUB + 1, 4, 65], BF16)
    va32 = sb("va32", [128, NSUB, 256], F32)
    pTs = sb("pTs", [128, 512], BF16)
    pTs2 = [pTs, sb("pTsB", [128, 512], BF16)]
    rec = sb("rec", [128, 4], F32); r2 = sb("r2", [128, 256], F32)
    gsil = sb("gsil", [128, 512], F32); gsig = sb("gsig", [128, 512], F32)
    t3 = gsil; t4 = gsig
    r3 = t1; r4 = t2
    stats = sb("stats", [128, 4, 6], F32); statsL = sb("statsL", [128, NSUB, 4, 6], F32); mv = sb("mv", [128, 2], F32); sd = sb("sd", [128, 1], F32); rstd = sb("rstd", [128, 1], F32)
    osb = sb("osb", [128, 256], F32)
    mhalf = sb("mhalf", [128, 1], F32)
    decT = sb("decT_s", [128, NH, 128], F32); qdec = sb("qdec_s", [128, NH, 128], F32); kdec = sb("kdec_s", [128, NH], F32)
    decTs = sb("decTs_s", [32, NH, 32], F32); qdecs = sb("qdecs_s", [128, NH, 2, 32], F32); kdecs = sb("kdecs_s", [32, NH, 2], F32)
    maskb = sb("maskb", [128, 2, 128], BF16); masksb = sb("masksb", [128, 3, 32], BF16)
    idb = sb("idb", [128, 128], BF16); id32 = sb("id32", [128, 128], F32); psw = sb("psw", [128, 128], F32)
    sel = sb("sel_s", [3, 160], F32)
    flag = sb("flag_s", [128, 1], F32)
    esink = sb("esink", [128, 32], F32)
    bc = {"gng": sb("bc_gng", [128, D], BF16)}
    bcb = sb("bcb", [128, D], BF16)
    VEC_ROW = {"l1g": 1, "l1b": 2, "l2g": 3, "l2b": 4}

    vecb = nc.dram_tensor("vecb", [5, D], BF16).ap()
    VB = Buf("vecb", None)

    def bc_load(name):
        i = VEC_ROW[name]
        if PM['q'] == 'sp':
            R.dma('sp', bcb.t[:], vecb[i:i + 1, :].partition_broadcast(128), reads=[VB], writes=[bcb])
        else:
            R.dma('pool', bcb.t[:], vec_d[i:i + 1, :].partition_broadcast(128), writes=[bcb])
        return bcb
    gtAp = sb("gtAp", [128, D], BF16); gtFp = sb("gtFp", [128, D], BF16)
    gtAs = gtAp; gtFs = gtFp
    cT = sb("cT_s", [128, KC, 3], F32); scT = sb("scT", [128, KC, 3], BF16)
    modT = sb("modT", [128, 64, 3], F32)
    badaT = sb("badaT", [128, 96], F32); lnT = sb("lnT_s", [128, 2, KC], F32)
    G2 = sb("G2", [128, KC, 3], F32); B2 = sb("B2", [128, KC, 3], F32)
    modP = sb("modP", [128, 32, 1], F32)
    gtrow = sb("gtrow", [3, 512], F32); brow = sb("brow", [3, 512], F32)
    ckTb = sb("ckTb", [128, 2, 2, 2, 128], BF16)
    cvext = sb("cvext", [128, 2, 4, 65], BF16)
    ktoks = sb("ktoks", [32, 2, 256], BF16)

    def mm(out, lhsT, rhs, start, stop, reads, writes):
        R.op('pe', lambda e: e.matmul(out, lhsT, rhs, start=start, stop=stop), reads, writes)

    def tp(out, in_, ident, reads, writes):
        R.op('pe', lambda e: e.transpose(out, in_, ident), reads, writes)

    def act(out, in_, func, reads, writes, bias=None, scale=None):
        kw = {}
        if bias is not None:
            kw['bias'] = bias
        if scale is not None:
            kw['scale'] = scale
        R.op('act', lambda e: e.activation(out, in_, func, **kw), reads, writes)

    def tt(out, in0, in1, op, reads, writes, eng='dve'):
        R.op(eng, lambda e: e.tensor_tensor(out, in0, in1, op), reads, writes)

    def ts(out, in0, s1, s2, op0, op1, reads, writes, eng='dve'):
        if op1 is None:
            R.op(eng, lambda e: e.tensor_scalar(out, in0, s1, None, op0), reads, writes)
        else:
            R.op(eng, lambda e: e.tensor_scalar(out, in0, s1, s2, op0, op1), reads, writes)

    def stt(out, in0, scalar, in1, op0, op1, reads, writes):
        R.op('dve', lambda e: e.scalar_tensor_tensor(out, in0, scalar, in1, op0, op1), reads, writes)

    def cp(out, in_, reads, writes, eng='dve'):
        R.op(eng, lambda e: e.tensor_copy(out, in_), reads, writes)

    def ld(out, in_, buf, q='sp'):
        R.dma(q, out, in_, writes=[buf])

    reg = {}
    order = []
    tot = [0]

    def register(key, src, nk, ncols, grp):
        reg[key] = (tot[0], nk, ncols, grp)
        order.append((key, src, nk, ncols, grp))
        tot[0] += nk * ncols

    for hp in range(4):
        register(('in', C_KR + hp * 512), win_d[:, C_KR + hp * 512: C_KR + (hp + 1) * 512], KC, 512, 0)
        register(('in', C_VR + hp * 512), win_d[:, C_VR + hp * 512: C_VR + (hp + 1) * 512], KC, 512, 0)
    register(('in', C_KA), win_d[:, C_KA: C_KA + 512], KC, 512, 0)
    register(('in', C_KALT), win_d[:, C_KALT: C_KALT + 256], KC, 256, 0)
    for x_ in range(4):
        register(('in', C_GTA + x_ * 512), win_d[:, C_GTA + x_ * 512: C_GTA + (x_ + 1) * 512], KC, 512, 1)
        register(('in', C_QA + x_ * 512), win_d[:, C_QA + x_ * 512: C_QA + (x_ + 1) * 512], KC, 512, 1)
    for hp in range(4):
        for c0 in (C_GR, C_GTR, C_QR):
            register(('in', c0 + hp * 512), win_d[:, c0 + hp * 512: c0 + (hp + 1) * 512], KC, 512, 2)
    for nn in range(4):
        register(('o', nn), wo_d[:, nn * 512:(nn + 1) * 512], KC, 512, 3)
    for f in range(11):
        register(('g', f), wg_d[:, f * 512:(f + 1) * 512], KC, 512, 4)
        register(('u', f), wu_d[:, f * 512:(f + 1) * 512], KC, 512, 4)
    for nn in range(4):
        for kg, nk in ((0, 16), (1, 16), (2, 12)):
            register(('d', kg, nn), wd_d[kg * 2048: kg * 2048 + nk * 128, nn * 512:(nn + 1) * 512], nk, 512, 5)
    wsc = nc.dram_tensor("wsc", [128, tot[0]], BF16).ap()
    WG = [Buf(f"wgrp{i}", None) for i in range(6)]

    def convert(groups):
        for (key, src, nk, ncols, grp) in order:
            if grp in groups:
                off = reg[key][0]
                R.dma('pool', wsc[:, off:off + nk * ncols].rearrange("p (k c) -> p k c", k=nk),
                      src.rearrange("(k p) c -> p k c", p=128), writes=[WG[grp]])

    def wload(key, src, nk, ncols):
        st8['w'] += 1
        wb = W[st8['w'] % NW]
        if key is None:
            R.dma('pool', wb.t[:, 0:nk, 0:ncols], src.rearrange("(k p) c -> p k c", p=128), writes=[wb])
        else:
            off, nk2, nc2, grp = reg[key]
            assert nk2 == nk and nc2 == ncols
            R.dma('sp', wb.t[:, 0:nk, 0:ncols], wsc[:, off:off + nk * ncols].rearrange("p (k c) -> p k c", k=nk), reads=[WG[grp]], writes=[wb])
        return wb

    R.dma('pool', vecb, vec_d, writes=[VB])
    convert((0,))
    ld(cT.t[:], cT_d.rearrange("p (k b) -> p k b", b=3), cT)
    ld(badaT.t[:], badaT_d, badaT)
    ld(lnT.t[:], lnT_d.rearrange("p (a k) -> p a k", a=2), lnT)
    ld(flag.t[:], flag_d, flag)
    ld(decT.t[:], decT_d.rearrange("p (h l) -> p h l", h=NH), decT)
    ld(qdec.t[:], qdec_d.rearrange("p (h l) -> p h l", h=NH), qdec)
    ld(kdec.t[:], kdec_d, kdec)
    ld(decTs.t[:], decTs_d.rearrange("p (h l) -> p h l", h=NH), decTs)
    ld(qdecs.t[:], qdecs_d.rearrange("p (h b l) -> p h b l", h=NH, b=2), qdecs)
    ld(kdecs.t[:], kdecs_d.rearrange("p (h b) -> p h b", h=NH), kdecs)
    ld(id32.t[:], id_d, id32)
    ld(psw.t[:], psw_d, psw)
    ld(sel.t[:], sel_d, sel)
    ld(maskb.t[:], mask_d.rearrange("p (a k) -> p a k", a=2), maskb, q='pool')
    ld(masksb.t[:], masks_d.rearrange("p (a k) -> p a k", a=3), masksb, q='pool')
    ld(idb.t[:], id_d, idb, q='pool')
    ld(bc["gng"].t[:], vec_d[0:1, :].partition_broadcast(128), bc["gng"], q='pool')
    ld(esink.t[:], sink_d.partition_broadcast(128), esink)
    act(esink.t[:], esink.t[:], AF.Exp, [esink], [esink])
    act(scT.t[:], cT.t[:], AF.Silu, [cT], [scT])

    gts = nc.dram_tensor("gts", [2, 32, D], BF16).ap()
    GS = Buf("gts", None)

    def build_gates():
        for j, (seg, dst) in enumerate(((2, gtAp), (5, gtFp))):
            for half in range(4):
                wt = wload(None, wada_d[:, seg * D + half * 512: seg * D + (half + 1) * 512], KC, 512)
                ld(brow.t[:], bada_d[0:1, seg * D + half * 512: seg * D + (half + 1) * 512].partition_broadcast(3), brow)
                p = nb()
                for k in range(KC):
                    mm(p.t[0:3, :], scT.t[:, k, :], wt.t[:, k, :], k == 0, k == KC - 1, [scT, wt], [p])
                tt(gtrow.t[:], p.t[0:3, :], brow.t[:], ALU.add, [p, brow], [gtrow])
                p2 = nb()
                mm(p2.t[:, :], sel.t[:, 0:128], gtrow.t[:], True, True, [sel, gtrow], [p2])
                act(dst.t[:, half * 512:(half + 1) * 512], p2.t[:, :], AF.Copy, [p2], [dst])
                p3 = nb()
                mm(p3.t[0:32, :], sel.t[:, 128:160], gtrow.t[:], True, True, [sel, gtrow], [p3])
                act(pTs.t[0:32, :], p3.t[0:32, :], AF.Copy, [p3], [pTs])
                R.dma('sp', gts[j, :, half * 512:(half + 1) * 512], pTs.t[0:32, :], reads=[pTs], writes=[GS])

    for seg in (0, 1, 3, 4):
        for half in range(4):
            wt = wload(None, wada_d[:, seg * D + half * 512: seg * D + (half + 1) * 512], KC, 512)
            base = {0: 0, 1: 16, 3: 32, 4: 48}[seg]
            p = nb()
            for c in range(4):
                for k in range(KC):
                    mm(p.t[:, c * 4:c * 4 + 3], wt.t[:, k, c * 128:(c + 1) * 128], scT.t[:, k, :], k == 0, k == KC - 1, [scT, wt], [p])
            for c in range(4):
                jj = seg * 16 + half * 4 + c
                ts(modT.t[:, base + half * 4 + c, :], p.t[:, c * 4:c * 4 + 3], badaT.t[:, jj:jj + 1], None, ALU.add, None, [p, badaT], [modT])
    ts(modT.t[:, 16:32, :], modT.t[:, 16:32, :], 1.0, None, ALU.add, None, [modT], [modT])
    ts(modT.t[:, 48:64, :], modT.t[:, 48:64, :], 1.0, None, ALU.add, None, [modT], [modT])
    for b in range(3):
        tt(G2.t[:, :, b], modT.t[:, 48:64, b], lnT.t[:, 0, :], ALU.mult, [modT, lnT], [G2])
        tt(B2.t[:, :, b], modT.t[:, 48:64, b], lnT.t[:, 1, :], ALU.mult, [modT, lnT], [B2])
        tt(B2.t[:, :, b], B2.t[:, :, b], modT.t[:, 32:48, b], ALU.add, [modT, B2], [B2])
    build_gates()
    convert((1, 2, 3, 4, 5))
    R.op('dve', lambda e: e.memset(vext.t[:], 1.0), [], [vext])
    R.op('dve', lambda e: e.memset(mhalf.t[:], -0.5), [], [mhalf])
    ts(bc["gng"].t[:], bc["gng"].t[:], 0.25, None, ALU.mult, None, [bc["gng"]], [bc["gng"]])
    R.op('dve', lambda e: e.memset(cvext.t[:], 1.0), [], [cvext])
    R.op('dve', lambda e: e.memset(S32.t[:], 0.0), [], [S32])
    R.op('dve', lambda e: e.memset(Sbf.t[:], 0.0), [], [Sbf])

    def load_x_make_hT(xsrc, ntok, bcols, xb=None, hdst=None, mt=None):
        xb = xb or xtok
        hdst = hdst or hT
        mt = mt or modT
        nsub = (ntok + 127) // 128
        for s in range(nsub):
            n = min(128, ntok - s * 128)
            ld(xb.t[0:n, s, :], xsrc[s * 128:s * 128 + n, :], xb)
        for k in range(KC):
            p = nb()
            for s in range(nsub):
                n = min(128, ntok - s * 128)
                tp(p.t[:, s * 128:s * 128 + n], xb.t[0:n, s, k * 128:(k + 1) * 128], id32.t[0:n, 0:n], [xb, id32], [p])
            for (c0, ncol, b) in bcols:
                act(hdst.t[:, k, c0:c0 + ncol], p.t[:, c0:c0 + ncol], AF.Identity, [p, mt], [hdst],
                    bias=mt.t[:, k, b:b + 1], scale=mt.t[:, 16 + k, b:b + 1])

    def wslot_as_x():
        st8['w'] += 1
        wb = W[st8['w'] % NW]
        v = Buf(wb.name, wb.t[:].rearrange("p k c -> p (k c)").bitcast(F32)[:, 0:NSUB * D].rearrange("p (s d) -> p s d", s=NSUB))
        v.writers = wb.writers
        v.readers = wb.readers
        return v

    def proj_fm(wt, c, src, ntok):
        p = nb()
        for k in range(KC):
            mm(p.t[:, 0:ntok], wt.t[:, k, c * 128:(c + 1) * 128], src.t[:, k, 0:ntok], k == 0, k == KC - 1, [wt, src], [p])
        return p

    def proj_tm(wt, s, n, src, c0, ncols):
        p = nb()
        for k in range(KC):
            mm(p.t[0:n, 0:ncols], src.t[:, k, s * 128:s * 128 + n], wt.t[:, k, c0:c0 + ncols], k == 0, k == KC - 1, [wt, src], [p])
        return p

    def rot_ret(pa, pb, dst, hh, ntok, ceng=None):
        ceng = ceng or PM['ce']
        cs, sn = rtab.t[:, 0, 0:ntok], rtab.t[:, 1, 0:ntok]
        tt(t1.t[:, 0:ntok], pa.t[:, 0:ntok], cs, ALU.mult, [pa, rtab], [t1])
        tt(t2.t[:, 0:ntok], pb.t[:, 0:ntok], sn, ALU.mult, [pb, rtab], [t2])
        tt(t3.t[:, 0:ntok], pa.t[:, 0:ntok], sn, ALU.mult, [pa, rtab], [t3])
        tt(t4.t[:, 0:ntok], pb.t[:, 0:ntok], cs, ALU.mult, [pb, rtab], [t4])
        tt(dst.t[:, hh, 0, 0:ntok], t1.t[:, 0:ntok], t2.t[:, 0:ntok], ALU.subtract, [t1, t2], [dst], eng=ceng)
        tt(dst.t[:, hh, 1, 0:ntok], t3.t[:, 0:ntok], t4.t[:, 0:ntok], ALU.add, [t3, t4], [dst], eng=ceng)

    def rot_att(p, ntok, out32=None, outb=None):
        act(q32.t[:, 0:ntok], p.t[:, 0:ntok], AF.Copy, [p], [q32])
        p2 = nb()
        mm(p2.t[:, 0:ntok], psw.t[:], q32.t[:, 0:ntok], True, True, [psw, q32], [p2])
        tt(t1.t[:, 0:ntok], q32.t[:, 0:ntok], atab.t[:, 0, 0:ntok], ALU.mult, [q32, atab], [t1])
        tt(t2.t[:, 0:ntok], p2.t[:, 0:ntok], atab.t[:, 1, 0:ntok], ALU.mult, [p2, atab], [t2])
        if out32 is not None:
            tt(out32, t1.t[:, 0:ntok], t2.t[:, 0:ntok], ALU.add, [t1, t2], [ka32])
            cp(outb, out32, [ka32], [kaT])
        else:
            tt(outb[0], t1.t[:, 0:ntok], t2.t[:, 0:ntok], ALU.add, [t1, t2], [outb[1]])

    def ret_kv_proj(hp, src, ntok, nsub_list, ceng='dve'):
        wt = wload(('in', C_KR + hp * 512), win_d[:, C_KR + hp * 512: C_KR + (hp + 1) * 512], KC, 512)
        for hh in range(2):
            pa = proj_fm(wt, 2 * hh, src, ntok)
            pb = proj_fm(wt, 2 * hh + 1, src, ntok)
            rot_ret(pa, pb, kT, hh, ntok, ceng)
        wt = wload(('in', C_VR + hp * 512), win_d[:, C_VR + hp * 512: C_VR + (hp + 1) * 512], KC, 512)
        for (s, n) in nsub_list:
            p = proj_tm(wt, s, n, src, 0, 512)
            act(vtok.t[0:n, s, :], p.t[0:n, :], AF.Copy, [p], [vtok])

    def state_update(h, hh, s):
        pt = nbt()
        for half in range(2):
            tp(pt.t[:, half * 128:(half + 1) * 128], kT.t[:, hh, half, s * 128:(s + 1) * 128], idb.t[:], [kT, idb], [pt])
        ts(ktok.t[:], pt.t[:, 0:256], kdec.t[:, h:h + 1], None, ALU.mult, None, [pt, kdec], [ktok])
        p = nb()
        for half in range(2):
            mm(p.t[:, half * 256:(half + 1) * 256], ktok.t[:, half * 128:(half + 1) * 128], vtok.t[:, s, hh * 256:(hh + 1) * 256], True, True, [ktok, vtok], [p])
        stt(S32.t[:, h, :, :], S32.t[:, h, :, :], GAM[h] ** 128, p.t[:].rearrange("p (a v) -> p a v", a=2), ALU.mult, ALU.add, [S32, p], [S32])
        act(Sbf.t[:, h, :, :], S32.t[:, h, :, :], AF.Copy, [S32], [Sbf])

    def attn_kv_proj(src, ntok, nsub_list, tabsrc, want_out):
        wt = wload(('in', C_KA), win_d[:, C_KA: C_KA + 512], KC, 512)
        wt2 = wload(('in', C_KALT), win_d[:, C_KALT: C_KALT + 256], KC, 256)
        for j in range(2):
            p = proj_fm(wt, j, src, ntok)
            rot_att(p, ntok, out32=ka32.t[:, j, 0:ntok], outb=kaT.t[:, 0, j, 128:128 + ntok])
        for j in range(2):
            p = proj_fm(wt2, j, src, ntok)
            rot_att(p, ntok, outb=(kaT.t[:, 1, j, 128:128 + ntok], kaT))
        for (s, n) in nsub_list:
            p = proj_tm(wt, s, n, src, 256, 256)
            act(va32.t[0:n, s, :], p.t[0:n, 0:256], AF.Copy, [p], [va32])
            cp(vext.t[0:n, 1 + s, :, 0:64], va32.t[0:n, s, :].rearrange("p (x d) -> p x d", x=4), [va32], [vext])

    def shift_prev(ntok):
        cp(kaT.t[:, :, :, 0:128], kaT.t[:, :, :, ntok:ntok + 128], [kaT], [kaT])
        cp(vext.t[:, 0, :, :], vext.t[:, ntok // 128, :, :], [vext], [vext])

    def ln_block_stats(s, n, q):
        R.op('dve', lambda e: e.bn_stats(statsL.t[0:n, s, q, :], xtok.t[0:n, s, q * 512:(q + 1) * 512]), [xtok], [statsL])

    def layer_norm_stats(s, n):
        R.op('dve', lambda e: e.bn_aggr(mv.t[0:n, :], statsL.t[0:n, s, :, :].rearrange("p a b -> p (a b)")), [statsL], [mv])
        rstd_from_mv(n)

    def rstd_from_mv(n):
        ts(sd.t[0:n, :], mv.t[0:n, 1:2], LN_EPS, None, ALU.add, None, [mv], [sd])
        if PM['ce'] == 'pool':
            tt(rstd.t[0:n, :], sd.t[0:n, :], mhalf.t[0:n, :], ALU.pow, [sd, mhalf], [rstd], eng='pool')
        else:
            act(sd.t[0:n, :], sd.t[0:n, :], AF.Sqrt, [sd], [sd])
            R.op('dve', lambda e: e.reciprocal(rstd.t[0:n, :], sd.t[0:n, :]), [sd], [rstd])

    def tail_phases(ntok, nsub_list, bcols, gtA, gtF, xsrc, ydst, mid_hook=None):
        for k in range(KC):
            pt = nbt()
            for (s, n) in nsub_list:
                tp(pt.t[:, s * 128:s * 128 + n], merged.t[0:n, s, k * 128:(k + 1) * 128], idb.t[0:n, 0:n], [merged, idb], [pt])
            act(mT.t[:, k, 0:ntok], pt.t[:, 0:ntok], AF.Copy, [pt], [mT])
        for (s, n) in nsub_list:
            ld(xtok.t[0:n, s, :], xsrc[s * 128:s * 128 + n, :], xtok)
        for nn in range(4):
            wt = wload(('o', nn), wo_d[:, nn * 512:(nn + 1) * 512], KC, 512)
            for (s, n) in nsub_list:
                p = proj_tm(wt, s, n, mT, 0, 512)
                tt(gsil.t[0:n, :], p.t[0:n, :], gtA.t[0:n, nn * 512:(nn + 1) * 512], ALU.mult, [p, gtA], [gsil])
                stt(xtok.t[0:n, s, nn * 512:(nn + 1) * 512], xtok.t[0:n, s, nn * 512:(nn + 1) * 512], ALPHA, gsil.t[0:n, :], ALU.mult, ALU.add, [xtok, gsil], [xtok])
                ln_block_stats(s, n, nn)
        for (s, n) in nsub_list:
            layer_norm_stats(s, n)
            ts(xtok.t[0:n, s, :], xtok.t[0:n, s, :], mv.t[0:n, 0:1], rstd.t[0:n, 0:1], ALU.subtract, ALU.mult, [xtok, mv, rstd], [xtok])
        for (s, n) in nsub_list:
            for k4 in range(4):
                p = nb()
                for kk in range(4):
                    k = k4 * 4 + kk
                    tp(p.t[:, kk * 128:kk * 128 + n], xtok.t[0:n, s, k * 128:(k + 1) * 128], id32.t[0:n, 0:n], [xtok, id32], [p])
                for kk in range(4):
                    k = k4 * 4 + kk
                    for (c0, ncol, b) in bcols:
                        lo, hi = max(c0, s * 128), min(c0 + ncol, s * 128 + n)
                        if hi > lo:
                            act(h2T.t[:, k, lo:hi], p.t[:, kk * 128 + lo - s * 128: kk * 128 + hi - s * 128], AF.Identity, [p, G2, B2], [h2T],
                                bias=B2.t[:, k, b:b + 1], scale=G2.t[:, k, b:b + 1])
        for nm, op_ in (("l1g", ALU.mult), ("l1b", ALU.add)):
            bb = bc_load(nm)
            for (s, n) in nsub_list:
                tt(xtok.t[0:n, s, :], xtok.t[0:n, s, :], bb.t[0:n, :], op_, [xtok, bb], [xtok], eng=PM['ce'])
        R.handoff(mixer_bufs, [ffT])
        for f in range(11):
            wg = wload(('g', f), wg_d[:, f * 512:(f + 1) * 512], KC, 512)
            wu = wload(('u', f), wu_d[:, f * 512:(f + 1) * 512], KC, 512)
            for c in range(4):
                pg = proj_fm(wg, c, h2T, ntok)
                pu = proj_fm(wu, c, h2T, ntok)
                act(t1.t[:, 0:ntok], pg.t[:, 0:ntok], AF.Tanh, [pg], [t1], scale=0.5)
                stt(t1.t[:, 0:ntok], t1.t[:, 0:ntok], 1.0, pg.t[:, 0:ntok], ALU.add, ALU.mult, [t1, pg], [t1])
                stt(ffT.t[:, f * 4 + c, 0:ntok], t1.t[:, 0:ntok], 0.5, pu.t[:, 0:ntok], ALU.mult, ALU.mult, [t1, pu], [ffT])
        if mid_hook is not None:
            mid_hook()
        for nn in range(4):
            accs = [nb() for _ in nsub_list]
            for kg, nk in ((0, 16), (1, 16), (2, 12)):
                wt = wload(('d', kg, nn), wd_d[kg * 2048: kg * 2048 + nk * 128, nn * 512:(nn + 1) * 512], nk, 512)
                for i, (s, n) in enumerate(nsub_list):
                    for k in range(nk):
                        kk = kg * 16 + k
                        mm(accs[i].t[0:n, :], ffT.t[:, kk, s * 128:s * 128 + n], wt.t[:, k, :], kk == 0, kk == 43, [ffT, wt], [accs[i]])
            for i, (s, n) in enumerate(nsub_list):
                tt(gsil.t[0:n, :], accs[i].t[0:n, :], gtF.t[0:n, nn * 512:(nn + 1) * 512], ALU.mult, [accs[i], gtF], [gsil])
                stt(xtok.t[0:n, s, nn * 512:(nn + 1) * 512], xtok.t[0:n, s, nn * 512:(nn + 1) * 512], ALPHA, gsil.t[0:n, :], ALU.mult, ALU.add, [xtok, gsil], [xtok])
                ln_block_stats(s, n, nn)
        R.handoff([ffT], mixer_bufs)
        for (s, n) in nsub_list:
            layer_norm_stats(s, n)
            ts(xtok.t[0:n, s, :], xtok.t[0:n, s, :], mv.t[0:n, 0:1], rstd.t[0:n, 0:1], ALU.subtract, ALU.mult, [xtok, mv, rstd], [xtok])
        for nm, op_ in (("l2g", ALU.mult), ("l2b", ALU.add)):
            bb = bc_load(nm)
            for (s, n) in nsub_list:
                tt(xtok.t[0:n, s, :], xtok.t[0:n, s, :], bb.t[0:n, :], op_, [xtok, bb], [xtok], eng=PM['ce'])
        for (s, n) in nsub_list:
            R.dma(PM['q'], ydst[s * 128:s * 128 + n, :], xtok.t[0:n, s, :], reads=[xtok])

    full_subs = [(s, 128) for s in range(NSUB)]

    RSTEPS = [(hh, s) for s in range(NSUB) for hh in range(2)]

    def ret_stageA(hp, with_q):
        for i, (hh, s) in enumerate(RSTEPS):
            h = 2 * hp + hh
            blk = slice(s * 128, (s + 1) * 128)
            pt = nbt()
            for half in range(2):
                tp(pt.t[:, half * 128:(half + 1) * 128], kT.t[:, hh, half, blk], idb.t[:], [kT, idb], [pt])
            act(ktokA.t[:, i, :], pt.t[:, 0:256], AF.Copy, [pt, kdec], [ktokA], scale=kdec.t[:, h:h + 1])
            if with_q:
                for half in range(2):
                    tt(qdT.t[:, hh, half, blk], qT.t[:, hh, half, blk], qdec.t[:, h, :], ALU.mult, [qT, qdec], [qdT], eng=PM['ce'])
                pi = nb()
                for half in range(2):
                    mm(pi.t[:, 0:128], kT.t[:, hh, half, blk], qT.t[:, hh, half, blk], half == 0, half == 1, [kT, qT], [pi])
                tt(iTA.t[:, i, :], pi.t[:, 0:128], decT.t[:, h, :], ALU.mult, [pi, decT], [iTA])

    def ret_stepB(hp, i, with_o):
        hh, s = RSTEPS[i]
        h = 2 * hp + hh
        blk = slice(s * 128, (s + 1) * 128)
        if with_o:
            po = nb()
            mm(po.t[:, 0:256], iTA.t[:, i, :], vtok.t[:, s, hh * 256:(hh + 1) * 256], True, False, [iTA, vtok], [po])
            for half in range(2):
                mm(po.t[:, 0:256], qdT.t[:, hh, half, blk], Sbf.t[:, h, half, :], False, half == 1, [qdT, Sbf], [po])
        p = nb()
        for half in range(2):
            mm(p.t[:, half * 256:(half + 1) * 256], ktokA.t[:, i, half * 128:(half + 1) * 128], vtok.t[:, s, hh * 256:(hh + 1) * 256], True, True, [ktokA, vtok], [p])
        stt(S32.t[:, h, :, :], S32.t[:, h, :, :], GAM[h] ** 128, p.t[:].rearrange("p (a v) -> p a v", a=2), ALU.mult, ALU.add, [S32, p], [S32])
        if with_o:
            act(Sbf.t[:, h, :, :], S32.t[:, h, :, :], AF.Copy, [S32], [Sbf])
            R.op('dve', lambda e, po=po: e.bn_stats(stats.t[:, 0, :], po.t[:, 0:256]), [po], [stats])
            R.op('dve', lambda e: e.bn_aggr(mv.t[:], stats.t[:, 0, :]), [stats], [mv])
            rstd_from_mv(128)
            c0 = h * 256
            stt(osb.t[:], po.t[:, 0:256], mv.t[:, 0:1], comb.t[:, s, hh * 256:hh * 256 + 256], ALU.subtract, ALU.mult, [po, mv, comb], [osb])
            stt(merged.t[:, s, c0:c0 + 256], osb.t[:], rstd.t[:, 0:1], merged.t[:, s, c0:c0 + 256], ALU.mult, ALU.add, [osb, rstd, merged], [merged])

    def retB_gen(hp):
        for i in range(len(RSTEPS)):
            ret_stepB(hp, i, True)
            yield

    def ret_proj(hp):
        wt = wload(('in', C_QR + hp * 512), win_d[:, C_QR + hp * 512: C_QR + (hp + 1) * 512], KC, 512)
        for hh in range(2):
            pa = proj_fm(wt, 2 * hh, hT, TT)
            pb = proj_fm(wt, 2 * hh + 1, hT, TT)
            rot_ret(pa, pb, qT, hh, TT)
        wt = wload(('in', C_KR + hp * 512), win_d[:, C_KR + hp * 512: C_KR + (hp + 1) * 512], KC, 512)
        for hh in range(2):
            pa = proj_fm(wt, 2 * hh, hT, TT)
            pb = proj_fm(wt, 2 * hh + 1, hT, TT)
            rot_ret(pa, pb, kT, hh, TT)
        wt = wload(('in', C_VR + hp * 512), win_d[:, C_VR + hp * 512: C_VR + (hp + 1) * 512], KC, 512)
        for (s, n) in full_subs:
            p = proj_tm(wt, s, n, hT, 0, 512)
            act(vtok.t[0:n, s, :], p.t[0:n, :], AF.Copy, [p], [vtok])
        ret_stageA(hp, True)
        wt = wload(('in', C_GR + hp * 512), win_d[:, C_GR + hp * 512: C_GR + (hp + 1) * 512], KC, 512)
        wt2 = wload(('in', C_GTR + hp * 512), win_d[:, C_GTR + hp * 512: C_GTR + (hp + 1) * 512], KC, 512)
        for (s, n) in full_subs:
            p = proj_tm(wt, s, n, hT, 0, 512)
            act(gsil.t[:], p.t[:], AF.Tanh, [p], [gsil], scale=0.5)
            stt(gsil.t[:], gsil.t[:], 1.0, p.t[:], ALU.add, ALU.mult, [gsil, p], [gsil])
            p2 = proj_tm(wt2, s, n, hT, 0, 512)
            act(gsig.t[:], p2.t[:], AF.Tanh, [p2], [gsig], scale=0.5)
            stt(gsil.t[:], gsig.t[:], 1.0, gsil.t[:], ALU.add, ALU.mult, [gsil, gsig], [gsil])
            tt(comb.t[:, s, :], gsil.t[:], bc["gng"].t[:, hp * 512:(hp + 1) * 512], ALU.mult, [gsil, bc["gng"]], [comb], eng=PM['ce'])

    def att_group(x_):
        wt = wload(('in', C_GTA + x_ * 512), win_d[:, C_GTA + x_ * 512: C_GTA + (x_ + 1) * 512], KC, 512)
        for (s, n) in full_subs:
            p = proj_tm(wt, s, n, hT, 0, 512)
            act(sga.t[:, s, :], p.t[:, :], AF.Tanh, [p], [sga], scale=0.5)
            yield
        wt = wload(('in', C_QA + x_ * 512), win_d[:, C_QA + x_ * 512: C_QA + (x_ + 1) * 512], KC, 512)

        def rot_fin(c):
            qb = q32s[c % 3]
            p2 = nb()
            mm(p2.t[:, 0:TT], psw.t[:], qb.t[:, 0:TT], True, True, [psw, qb], [p2])
            tt(r3.t[:, 0:TT], qb.t[:, 0:TT], atab.t[:, 0, 0:TT], ALU.mult, [qb, atab], [r3], eng=PM['ce'])
            tt(r4.t[:, 0:TT], p2.t[:, 0:TT], atab.t[:, 1, 0:TT], ALU.mult, [p2, atab], [r4])
            tt(qaTs[c].t[:, 0:TT], r3.t[:, 0:TT], r4.t[:, 0:TT], ALU.add, [r3, r4], [qaTs[c]], eng=PM['ce'])

        for c in range(4):
            p = proj_fm(wt, c, hT, TT)
            act(q32s[c % 3].t[:, 0:TT], p.t[:, 0:TT], AF.Copy, [p], [q32s[c % 3]])
            if c > 1:
                rot_fin(c - 2)
            yield
        rot_fin(2)
        yield
        rot_fin(3)
        yield
        units = [(s, half, gp) for s in range(NSUB) for half in range(2) for gp in range(2)]
        pos = {}

        def scores(i):
            s, half, gp = units[i]
            psc = nb()
            for g2 in range(2):
                g = half * 4 + gp * 2 + g2
                off = (g % 2) * 64
                ver = 0 if (x_ % 2) == (g % 2) else 1
                j = x_ // 2
                cb = g2 * 256
                for kb in range(2):
                    kc0 = s * 128 + kb * 128
                    mm(psc.t[:, cb + kb * 128: cb + (kb + 1) * 128], kaT.t[off:off + 64, ver, j, kc0:kc0 + 128],
                       qaTs[g // 2].t[off:off + 64, s * 128:(s + 1) * 128], True, False, [kaT, qaTs[g // 2]], [psc])
                    mm(psc.t[:, cb + kb * 128: cb + (kb + 1) * 128], idb.t[:], maskb.t[:, kb, :], False, True, [idb, maskb], [psc])
            act(pTs2[i % 2].t[:], psc.t[:], AF.Exp, [psc], [pTs2[i % 2]], scale=0.125)

        def pv(i):
            s, half, gp = units[i]
            if gp == 0:
                st8['po'] += 1
                pos[(s, half)] = PSA[st8['po'] % 2]
            po = pos[(s, half)]
            pt_ = pTs2[i % 2]
            for g2 in range(2):
                gg = gp * 2 + g2
                for kb in range(2):
                    mm(po.t[:, gg * 65:(gg + 1) * 65], pt_.t[:, g2 * 256 + kb * 128: g2 * 256 + (kb + 1) * 128],
                       vext.t[:, s + kb, x_, :], kb == 0, kb == 1, [pt_, vext], [po])
            if gp == 1:
                h0 = x_ * 8 + half * 4
                pov = po.t[:, 0:260].rearrange("p (g d) -> p g d", g=4)
                tt(rec.t[:], pov[:, :, 64], esink.t[:, h0:h0 + 4], ALU.add, [po, esink], [rec])
                R.op('dve', lambda e: e.reciprocal(rec.t[:], rec.t[:]), [rec], [rec])
                ts(rec.t[:], rec.t[:], 0.5, None, ALU.mult, None, [rec], [rec])
                c0 = h0 * 64
                stt(r2.t[:].rearrange("p (g d) -> p g d", g=4), sga.t[:, s, half * 256:half * 256 + 256].rearrange("p (g d) -> p g d", g=4), 1.0,
                    rec.t[:].unsqueeze(2).broadcast_to([128, 4, 64]), ALU.add, ALU.mult, [sga, rec], [r2])
                tt(merged.t[:, s, c0:c0 + 256].rearrange("p (g d) -> p g d", g=4), pov[:, :, 0:64],
                   r2.t[:].rearrange("p (g d) -> p g d", g=4), ALU.mult, [po, r2], [merged])

        scores(0)
        for i in range(len(units)):
            if i + 1 < len(units):
                scores(i + 1)
            pv(i)
            yield

    def interleave(g1, g2):
        gens = [g for g in (g1, g2) if g is not None]
        while gens:
            for g in list(gens):
                try:
                    next(g)
                except StopIteration:
                    gens.remove(g)

    xpre = Buf("xpre", A1[:, 0:NSUB * 2 * D].bitcast(F32).rearrange("p (s d) -> p s d", s=NSUB))
    hT2 = Buf("hT2", xtok.t[:].rearrange("p s d -> p (s d)").bitcast(BF16)[:, 0:KC * TT].rearrange("p (k t) -> p k t", k=KC))
    hbufs = [hT, hT2]
    ts(modP.t[:, :, 0], modT.t[:, 0:32, 0], flag.t[:, 0:1], None, ALU.mult, None, [modT, flag], [modP])
    load_x_make_hT(xp_d[0:TT, :], TT, [(0, TT, 0)], xb=xpre, hdst=hbufs[0], mt=modP)
    for p_ in range(NP):
        if p_ + 1 < NP:
            load_x_make_hT(xp_d[(p_ + 1) * TT:(p_ + 2) * TT, :], TT, [(0, TT, 0)], xb=xpre, hdst=hbufs[(p_ + 1) % 2], mt=modP)
        hsrc = hbufs[p_ % 2]
        ld(rtab.t[:], rtabp_d[:, :, p_ * TT:(p_ + 1) * TT].rearrange("a p t -> p a t"), rtab)
        for hp in range(4):
            ret_kv_proj(hp, hsrc, TT, full_subs)
            ret_stageA(hp, False)
            for i in range(len(RSTEPS)):
                ret_stepB(hp, i, False)
        if p_ == NP - 1:
            ld(atab.t[:], atabp_d[:, :, p_ * TT:(p_ + 1) * TT].rearrange("a p t -> p a t"), atab)
            attn_kv_proj(hsrc, TT, full_subs, None, False)
            shift_prev(TT)
    R.handoff([xpre], [merged, comb, sga, qT])
    R.handoff([hT2], [xtok])
    ts(S32.t[:].rearrange("p h a v -> p (h a v)"), S32.t[:].rearrange("p h a v -> p (h a v)"), flag.t[:, 0:1], None, ALU.mult, None, [S32, flag], [S32])
    act(Sbf.t[:].rearrange("p h a v -> p (h a v)"), S32.t[:].rearrange("p h a v -> p (h a v)"), AF.Copy, [S32], [Sbf])
    ts(vext.t[:, 0, :, :], vext.t[:, 0, :, :], flag.t[:, 0:1], None, ALU.mult, None, [vext, flag], [vext])

    def make_main_hT(p_):
        load_x_make_hT(x_d[p_ * TT:(p_ + 1) * TT, :], TT, [(0, TT, 0)], xb=wslot_as_x(), hdst=hT)

    make_main_hT(0)
    for p_ in range(NP):
        PM['ce'], PM['q'] = ('dve', 'sp') if p_ == 0 else ('pool', 'pool')
        xsrc = x_d[p_ * TT:(p_ + 1) * TT, :]
        ld(rtab.t[:], rtab_d[:, :, p_ * TT:(p_ + 1) * TT].rearrange("a p t -> p a t"), rtab)
        ld(atab.t[:], atab_d[:, :, p_ * TT:(p_ + 1) * TT].rearrange("a p t -> p a t"), atab)
        attn_kv_proj(hT, TT, full_subs, None, p_ == NP - 1)
        if p_ == NP - 1:
            p = nb()
            for j in range(2):
                tp(p.t[:, j * 128:(j + 1) * 128], ka32.t[:, j, TT - 128:TT], id32.t[:], [ka32, id32], [p])
            act(osb.t[:], p.t[:, 0:256], AF.Copy, [p], [osb])
            R.dma('sp', ko_d, osb.t[:], reads=[osb])
            R.dma('sp', vo_d, va32.t[:, NSUB - 1, :], reads=[va32])
        interleave(att_group(0), None)
        for hp in range(4):
            ret_proj(hp)
            interleave(retB_gen(hp), att_group(hp + 1) if hp < 3 else None)
        shift_prev(TT)
        tail_phases(TT, full_subs, [(0, TT, 0)], gtAp, gtFp, xsrc, y_d[p_ * TT:(p_ + 1) * TT, :],
                    mid_hook=(lambda p_=p_: make_main_hT(p_ + 1)) if p_ + 1 < NP else None)
    R.dma('sp', so_d.rearrange("h (a p) v -> p h a v", p=128), S32.t[:], reads=[S32])

    NS = 32
    ssub = [(0, NS)]
    sb_cols = [(0, 16, 1), (16, 16, 2)]
    ld(ckTb.t[:], ckT_d.rearrange("p (b v j k) -> p b v j k", b=2, v=2, j=2), ckTb, q='pool')
    ld(cvext.t[:, :, :, 0:64], cv_d.rearrange("p (b x d) -> p b x d", b=2, x=4), cvext, q='pool')
    for j_, dst_ in ((0, gtAp), (1, gtFp)):
        R.dma('sp', dst_.t[0:32, :], gts[j_], reads=[GS], writes=[dst_])
    load_x_make_hT(xs_d, NS, sb_cols)
    ld(rtab.t[:, :, 0:NS], rtabs_d.rearrange("a p t -> p a t"), rtab)
    ld(atab.t[:, :, 0:NS], atabs_d.rearrange("a p t -> p a t"), atab)
    wt = wload(('in', C_KA), win_d[:, C_KA: C_KA + 512], KC, 512)
    wt2 = wload(('in', C_KALT), win_d[:, C_KALT: C_KALT + 256], KC, 256)
    for j in range(2):
        p = proj_fm(wt, j, hT, NS)
        rot_att(p, NS, out32=ka32.t[:, j, 0:NS], outb=kaT.t[:, 0, j, 128:128 + NS])
    for j in range(2):
        p = proj_fm(wt2, j, hT, NS)
        rot_att(p, NS, outb=(kaT.t[:, 1, j, 128:128 + NS], kaT))
    p = proj_tm(wt, 0, NS, hT, 256, 256)
    act(va32.t[0:NS, 0, :], p.t[0:NS, 0:256], AF.Copy, [p], [va32])
    cp(vext.t[0:NS, 1, :, 0:64], va32.t[0:NS, 0, :].rearrange("p (x d) -> p x d", x=4), [va32], [vext])
    p = nb()
    for j in range(2):
        tp(p.t[0:NS, j * 128:(j + 1) * 128], ka32.t[:, j, 0:NS], id32.t[:], [ka32, id32], [p])
    act(osb.t[0:NS, :], p.t[0:NS, 0:256], AF.Copy, [p], [osb])
    R.dma('sp', kso_d, osb.t[0:NS, :], reads=[osb])
    R.dma('sp', vso_d, va32.t[0:NS, 0, :], reads=[va32])
    for x_ in range(4):
        wt = wload(('in', C_GTA + x_ * 512), win_d[:, C_GTA + x_ * 512: C_GTA + (x_ + 1) * 512], KC, 512)
        p = proj_tm(wt, 0, NS, hT, 0, 512)
        act(sga.t[0:NS, 0, :], p.t[0:NS, :], AF.Sigmoid, [p], [sga])
        wt = wload(('in', C_QA + x_ * 512), win_d[:, C_QA + x_ * 512: C_QA + (x_ + 1) * 512], KC, 512)
        for c in range(4):
            p = proj_fm(wt, c, hT, NS)
            rot_att(p, NS, outb=(qaTs[c].t[:, 0:NS], qaTs[c]))
        for half in range(2):
            po = PSA[half]
            for g4 in range(4):
                g = half * 4 + g4
                off = (g % 2) * 64
                ver = 0 if (x_ % 2) == (g % 2) else 1
                j = x_ // 2
                psc = nb()
                for kb in range(2):
                    mm(psc.t[:, kb * 32:(kb + 1) * 32], ckTb.t[off:off + 64, kb, ver, j, :], qaTs[g // 2].t[off:off + 64, 0:NS], True, False, [ckTb, qaTs[g // 2]], [psc])
                    mm(psc.t[:, kb * 32:(kb + 1) * 32], idb.t[:], masksb.t[:, kb, :], False, True, [idb, masksb], [psc])
                mm(psc.t[0:NS, 64:96], kaT.t[off:off + 64, ver, j, 128:128 + NS], qaTs[g // 2].t[off:off + 64, 0:NS], True, False, [kaT, qaTs[g // 2]], [psc])
                mm(psc.t[0:NS, 64:96], idb.t[0:NS, 0:NS], masksb.t[0:NS, 2, :], False, True, [idb, masksb], [psc])
                act(pTs.t[:, 0:64], psc.t[:, 0:64], AF.Exp, [psc], [pTs], scale=0.125)
                act(pTs.t[0:NS, 64:96], psc.t[0:NS, 64:96], AF.Exp, [psc], [pTs], scale=0.125)
                for kb in range(2):
                    mm(po.t[0:NS, g4 * 65:(g4 + 1) * 65], pTs.t[:, kb * 32:(kb + 1) * 32], cvext.t[:, kb, x_, :], kb == 0, False, [pTs, cvext], [po])
                mm(po.t[0:NS, g4 * 65:(g4 + 1) * 65], pTs.t[0:NS, 64:96], vext.t[0:NS, 1, x_, :], False, True, [pTs, vext], [po])
            h0 = x_ * 8 + half * 4
            pov = po.t[0:NS, 0:260].rearrange("p (g d) -> p g d", g=4)
            tt(rec.t[0:NS, :], pov[:, :, 64], esink.t[0:NS, h0:h0 + 4], ALU.add, [po, esink], [rec])
            R.op('dve', lambda e: e.reciprocal(rec.t[0:NS, :], rec.t[0:NS, :]), [rec], [rec])
            c0 = h0 * 64
            tt(r2.t[0:NS, :].rearrange("p (g d) -> p g d", g=4), sga.t[0:NS, 0, half * 256:half * 256 + 256].rearrange("p (g d) -> p g d", g=4),
               rec.t[0:NS, :].unsqueeze(2).broadcast_to([NS, 4, 64]), ALU.mult, [sga, rec], [r2])
            tt(merged.t[0:NS, 0, c0:c0 + 256].rearrange("p (g d) -> p g d", g=4), pov[:, :, 0:64],
               r2.t[0:NS, :].rearrange("p (g d) -> p g d", g=4), ALU.mult, [po, r2], [merged])
    for hp in range(4):
        wt = wload(('in', C_GR + hp * 512), win_d[:, C_GR + hp * 512: C_GR + (hp + 1) * 512], KC, 512)
        wt2 = wload(('in', C_GTR + hp * 512), win_d[:, C_GTR + hp * 512: C_GTR + (hp + 1) * 512], KC, 512)
        p = proj_tm(wt, 0, NS, hT, 0, 512)
        act(gsil.t[0:NS, :], p.t[0:NS, :], AF.Silu, [p], [gsil])
        p2 = proj_tm(wt2, 0, NS, hT, 0, 512)
        act(gsig.t[0:NS, :], p2.t[0:NS, :], AF.Sigmoid, [p2], [gsig])
        tt(gsil.t[0:NS, :], gsil.t[0:NS, :], gsig.t[0:NS, :], ALU.mult, [gsil, gsig], [gsil])
        stt(comb.t[0:NS, 0, :], gsil.t[0:NS, :], 4.0, bc["gng"].t[0:NS, hp * 512:(hp + 1) * 512], ALU.mult, ALU.mult, [gsil, bc["gng"]], [comb])
        wt = wload(('in', C_QR + hp * 512), win_d[:, C_QR + hp * 512: C_QR + (hp + 1) * 512], KC, 512)
        for hh in range(2):
            pa = proj_fm(wt, 2 * hh, hT, NS)
            pb = proj_fm(wt, 2 * hh + 1, hT, NS)
            rot_ret(pa, pb, qT, hh, NS)
        ret_kv_proj(hp, hT, NS, ssub)
        for bi in range(2):
            ld(S32.t[:, bi * 2:bi * 2 + 2, :, :], st_d[bi, 2 * hp:2 * hp + 2].rearrange("h (a p) v -> p h a v", p=128), S32)
        act(Sbf.t[:, 0:4, :, :], S32.t[:, 0:4, :, :], AF.Copy, [S32], [Sbf])
        for hh in range(2):
            h = 2 * hp + hh
            pi = nb()
            for half in range(2):
                mm(pi.t[0:NS, 0:NS], kT.t[:, hh, half, 0:NS], qT.t[:, hh, half, 0:NS], half == 0, half == 1, [kT, qT], [pi])
            tt(iT.t[0:NS, 0:NS], pi.t[0:NS, 0:NS], decTs.t[:, h, :], ALU.mult, [pi, decTs], [iT])
            po = nb()
            mm(po.t[0:NS, 0:256], iT.t[0:NS, 0:NS], vtok.t[0:NS, 0, hh * 256:(hh + 1) * 256], True, False, [iT, vtok], [po])
            for bi in range(2):
                for half in range(2):
                    tt(qdT.t[:, hh, half, 0:NS], qT.t[:, hh, half, 0:NS], qdecs.t[:, h, bi, :], ALU.mult, [qT, qdecs], [qdT])
                    mm(po.t[0:NS, 0:256], qdT.t[:, hh, half, 0:NS], Sbf.t[:, bi * 2 + hh, half, :], False, (bi == 1 and half == 1), [qdT, Sbf], [po])
            R.op('dve', lambda e, po=po: e.bn_stats(stats.t[0:NS, 0, :], po.t[0:NS, 0:256]), [po], [stats])
            R.op('dve', lambda e: e.bn_aggr(mv.t[0:NS, :], stats.t[0:NS, 0, :]), [stats], [mv])
            rstd_from_mv(NS)
            c0 = h * 256
            stt(osb.t[0:NS, :], po.t[0:NS, 0:256], mv.t[0:NS, 0:1], comb.t[0:NS, 0, hh * 256:hh * 256 + 256], ALU.subtract, ALU.mult, [po, mv, comb], [osb])
            stt(merged.t[0:NS, 0, c0:c0 + 256], osb.t[0:NS, :], rstd.t[0:NS, 0:1], merged.t[0:NS, 0, c0:c0 + 256], ALU.mult, ALU.add, [osb, rstd, merged], [merged])
            pt = nbt()
            for half in range(2):
                tp(pt.t[0:NS, half * 128:(half + 1) * 128], kT.t[:, hh, half, 0:NS], idb.t[:], [kT, idb], [pt])
            for bi in range(2):
                ts(ktoks.t[:, bi, :], pt.t[0:NS, 0:256], kdecs.t[:, h, bi:bi + 1], None, ALU.mult, None, [pt, kdecs], [ktoks])
                p = nb()
                for half in range(2):
                    mm(p.t[:, half * 256:(half + 1) * 256], ktoks.t[:, bi, half * 128:(half + 1) * 128], vtok.t[0:NS, 0, hh * 256:(hh + 1) * 256], True, True, [ktoks, vtok], [p])
                stt(S32.t[:, bi * 2 + hh, :, :], S32.t[:, bi * 2 + hh, :, :], GAM[h] ** 16, p.t[:].rearrange("p (a v) -> p a v", a=2), ALU.mult, ALU.add, [S32, p], [S32])
        for bi in range(2):
            R.dma('sp', sso_d[bi, 2 * hp:2 * hp + 2].rearrange("h (a p) v -> p h a v", p=128), S32.t[:, bi * 2:bi * 2 + 2, :, :], reads=[S32])
    tail_phases(NS, ssub, sb_cols, gtAs, gtFs, xs_d, ys_d)
    R.finish()
    return nc


def _rot_tables(pos):
    pos = pos.astype(np.float32)
    inv_r = (1.0 / (np.float32(10000.0) ** (np.arange(128, dtype=np.float32) / np.float32(128)))).astype(np.float32)
    ang = (pos[None, :] * inv_r[:, None]).astype(np.float32)
    rt = np.stack([np.cos(ang), np.sin(ang)]).astype(np.float32)
    inv_a = (1.0 / (np.float32(500000.0) ** (np.arange(8, dtype=np.float32) / np.float32(8)))).astype(np.float32)
    anga = (pos[None, :] * inv_a[:, None]).astype(np.float32)
    ca, sa = np.cos(anga).astype(np.float32), np.sin(anga).astype(np.float32)
    C = np.ones((64, pos.shape[0]), np.float32); S = np.zeros((64, pos.shape[0]), np.float32)
    C[0:8] = ca; C[8:16] = ca; S[0:8] = -sa; S[8:16] = sa
    at = np.stack([np.concatenate([C, C]), np.concatenate([S, S])]).astype(np.float32)
    return rt, at


def _const_tables():
    g = np.array(GAM, np.float64)
    l = np.arange(128)
    diff = l[None, :] - l[:, None]
    decT = np.where(diff[:, None, :] >= 0, g[None, :, None] ** np.maximum(diff, 0)[:, None, :], 0.0) / 16.0
    qdec = np.broadcast_to((g[:, None] ** (l[None, :] + 1.0))[None], (128, NH, 128))
    kdec = (g[None, :] ** (127.0 - l[:, None])) / 16.0
    t = np.arange(32)
    same = (t[:, None] // 16) == (t[None, :] // 16)
    d2 = (t[None, :] % 16) - (t[:, None] % 16)
    decTs = np.where((same & (d2 >= 0))[:, None, :], g[None, :, None] ** np.maximum(d2, 0)[:, None, :], 0.0) / 16.0
    qdecs = np.zeros((128, NH, 2, 32)); kdecs = np.zeros((32, NH, 2))
    for bi in range(2):
        m = (t // 16) == bi
        qdecs[:, :, bi, :] = (g[:, None] ** (t[None, :] % 16 + 1.0)) * m[None, :]
        kdecs[:, :, bi] = (g[None, :] ** (15.0 - (t[:, None] % 16))) / 16.0 * m[:, None]
    k = np.arange(128)
    maskP = np.where((k[:, None] < 64) & (k[None, :] >= 64), MASKV, 0.0)
    maskO = np.where((k[:, None] >= 64) & (k[None, :] < 64), MASKV, 0.0)
    masks = np.concatenate([maskP, maskO], axis=1)
    ms = np.zeros((128, 3, 32))
    for bi in range(2):
        ms[:, bi, :] = np.where((t[None, :] // 16) == bi, 0.0, MASKV)
    ms[0:32, 2, :] = np.where(same, 0.0, MASKV)
    ident = np.eye(128)
    psw = np.zeros((128, 128))
    for base in (0, 64):
        for i in range(8):
            psw[base + i + 8, base + i] = 1.0
            psw[base + i, base + i + 8] = 1.0
    sel = np.zeros((3, 160)); sel[0, 0:128] = 1.0; sel[1, 128:144] = 1.0; sel[2, 144:160] = 1.0
    f = lambda a: np.ascontiguousarray(a, dtype=np.float32)
    return dict(decT=f(decT.reshape(128, -1)), qdec=f(qdec.reshape(128, -1)), kdec=f(kdec), decTs=f(decTs.reshape(32, -1)),
                qdecs=f(qdecs.reshape(128, -1)), kdecs=f(kdecs.reshape(32, -1)), masks=f(masks), maskss=f(ms.reshape(128, -1)),
                ident=f(ident), pswap=f(psw), sel=f(sel))


def run(inputs, NP, NSUB, n_cores, trace=False):
    f = lambda a: np.ascontiguousarray(np.asarray(a), dtype=np.float32)
    xpr = f(inputs["x_prompt"]); xsm = f(inputs["x_sample"])
    TC = NP * NSUB * 128
    w_in = f(inputs["w_in"])[0]
    ka = w_in[:, C_KA:C_KA + 256].reshape(D, 4, 64)[:, [1, 0, 3, 2], :].reshape(D, 256)
    shared = dict(
        w_ada=f(inputs["w_ada"])[0], b_adaT=f(f(inputs["b_ada"])[0].reshape(96, 128).T), b_ada=f(inputs["b_ada"]),
        w_in=f(np.concatenate([w_in, ka], axis=1)), w_o=f(inputs["w_o"])[0], w_g=f(inputs["w_ffn_gate"])[0],
        w_u=f(inputs["w_ffn_up"])[0], w_d=f(inputs["w_ffn_down"])[0],
        vecs=f(np.stack([f(inputs[n])[0] for n in ("gn_g", "ln1_g", "ln1_b", "ln2_g", "ln2_b")])),
        lnT=f(np.concatenate([f(inputs["ln1_g"])[0].reshape(KC, 128).T, f(inputs["ln1_b"])[0].reshape(KC, 128).T], axis=1)),
        sinks=f(inputs["attn_sinks"]),
    )
    shared.update(_const_tables())
    rts, ats = _rot_tables(PAST + (np.arange(32) % 16))
    shared["rtabs"] = rts; shared["atabs"] = ats
    ck = f(inputs["cache_attn_k"])[0]; cvv = f(inputs["cache_attn_v"])[0]; stt_ = f(inputs["state_ret"])[0]
    cpr = f(inputs["c_prompt"]); csm = f(inputs["c_sample"])
    in_maps = []
    for c in range(n_cores):
        b, hf = c // 2, c % 2
        m = dict(shared)
        m["x"] = f(xpr[b, hf * TC:(hf + 1) * TC])
        m["xp"] = f(xpr[b, 0:TC]) if hf == 1 else np.zeros((TC, D), np.float32)
        m["xs"] = f(xsm[2 * c:2 * c + 2].reshape(32, D))
        m["flag"] = np.full((128, 1), float(hf), np.float32)
        cc = np.stack([cpr[b], csm[2 * c], csm[2 * c + 1]])
        m["cT"] = f(cc.reshape(3, KC, 128).transpose(2, 1, 0).reshape(128, KC * 3))
        rt, at = _rot_tables(hf * TC + np.arange(TC)); m["rtab"] = rt; m["atab"] = at
        rt, at = _rot_tables(np.arange(TC)); m["rtabp"] = rt; m["atabp"] = at
        ckc = np.zeros((128, 2, 2, 2, 128), np.float32)
        for bi in range(2):
            kk = ck[2 * c + bi]
            for ver in range(2):
                order = [0, 1, 2, 3] if ver == 0 else [1, 0, 3, 2]
                kt = kk[:, order, :].reshape(128, 2, 128)
                ckc[:, bi, ver, :, :] = kt.transpose(2, 1, 0)
        m["ckT"] = f(ckc.reshape(128, -1))
        m["cv"] = f(np.stack([cvv[2 * c + bi].reshape(128, 256) for bi in range(2)], axis=1).reshape(128, 512))
        m["state"] = f(stt_[2 * c:2 * c + 2])
        in_maps.append(m)
    nc = build(NP, NSUB)
    res = run_bass_kernel_spmd(nc, in_maps, core_ids=list(range(n_cores)), trace=trace)
    return res


def assemble(res, n_cores, TC):
    r = res.results
    nb_ = n_cores // 2
    y_p = np.zeros((nb_, 2 * TC, D), np.float32)
    y_s = np.zeros((2 * n_cores, 16, D), np.float32)
    kp = np.zeros((1, nb_, 128, 4, 64), np.float32); vp = np.zeros((1, nb_, 128, 4, 64), np.float32)
    sp = np.zeros((1, nb_, NH, 256, 256), np.float32)
    ks = np.zeros((1, 2 * n_cores, 16, 4, 64), np.float32); vs = np.zeros((1, 2 * n_cores, 16, 4, 64), np.float32)
    ss = np.zeros((1, 2 * n_cores, NH, 256, 256), np.float32)
    for c in range(n_cores):
        b, hf = c // 2, c % 2
        y_p[b, hf * TC:(hf + 1) * TC] = r[c]["y"]
        y_s[2 * c:2 * c + 2] = r[c]["ys"].reshape(2, 16, D)
        ks[0, 2 * c:2 * c + 2] = r[c]["ksout"].reshape(2, 16, 4, 64)
        vs[0, 2 * c:2 * c + 2] = r[c]["vsout"].reshape(2, 16, 4, 64)
        ss[0, 2 * c:2 * c + 2] = r[c]["ssout"]
        if hf == 1:
            kp[0, b] = r[c]["kout"].reshape(128, 4, 64)
            vp[0, b] = r[c]["vout"].reshape(128, 4, 64)
            sp[0, b] = r[c]["sout"]
    return (y_p, y_s, kp, vp, sp, ks, vs, ss)


def kernel(**inputs):
    NSUB = 2
    NP = 4096 // (NSUB * 128)
    res = run(inputs, NP, NSUB, 8)
    return assemble(res, 8, 4096)
```
